# Optimizing a Trainium2 kernel written in Bass

```python
import jax, jax.numpy as jnp
from jax import lax
import numpy as np

D_MODEL = 4096
BATCH = 2
SEQ = 8192
DEPTH = 1
DEC_BATCH = 8
DEC_SEQ = 32
PAST_LEN = 1024

CHUNK = 64
HG_HEADS = 16
HG_DK = 128
HG_DV = 128
HG_WIDTH = HG_HEADS * HG_DK
CONV_WIDTH = 2048
CONV_K = 3
N_MEM = 256
XA_HEADS = 4
XA_HEAD_DIM = D_MODEL // XA_HEADS
FFN_HIDDEN = ((8 * D_MODEL + 3 * 256 - 1) // (3 * 256)) * 256
PROJ_WIDTH = 4 * HG_WIDTH + 3 * CONV_WIDTH + 2 * D_MODEL
EPS = 1e-6

kernel_name = 'hgrn2_shortconv_gated_streaming_encoder_step'


def rmsnorm(x, g):
    xf = x.astype(jnp.float32)
    y = xf * lax.rsqrt(jnp.mean(xf * xf, axis=-1, keepdims=True) + EPS)
    return (y * g.astype(jnp.float32)).astype(x.dtype)


def hgrn2_recurrence(q, k, v, logf, s0):
    bsz, L, H, _ = q.shape
    C = min(CHUNK, L)
    n = L // C

    def to_chunks(t):
        return t.reshape(bsz, n, C, H, t.shape[-1]).transpose(1, 0, 3, 2, 4)

    causal = jnp.tril(jnp.ones((C, C), dtype=bool))[:, :, None]

    def step(S, inp):
        qc, kc, vc, gc = inp
        b = jnp.cumsum(gc, axis=2)
        o_inter = jnp.einsum('bhtk,bhkv->bhtv', qc * jnp.exp(b), S)
        diff = b[:, :, :, None, :] - b[:, :, None, :, :]
        decay = jnp.exp(jnp.where(causal, diff, -jnp.inf))
        scores = jnp.einsum('bhtk,bhtsk,bhsk->bhts', qc, decay, kc)
        o_intra = jnp.einsum('bhts,bhsv->bhtv', scores, vc)
        b_last = b[:, :, -1:, :]
        S_new = jnp.exp(b_last[:, :, 0, :])[..., None] * S + jnp.einsum('bhsk,bhsv->bhkv', kc * jnp.exp(b_last - b), vc)
        return S_new, o_inter + o_intra

    S_fin, o = lax.scan(step, s0, (to_chunks(q), to_chunks(k), to_chunks(v), to_chunks(logf)))
    o = o.transpose(1, 0, 3, 2, 4).reshape(bsz, L, H, v.shape[-1])
    return o, S_fin


def gated_mixer(h, w_in, lb, hg_norm, conv_w, w_a, w_b, w_o, S0, conv_buf):
    bsz, L, _ = h.shape
    f32 = jnp.float32
    sizes = [HG_WIDTH] * 4 + [CONV_WIDTH] * 3 + [D_MODEL] * 2
    idx = np.cumsum(sizes)[:-1].tolist()
    q, fpre, iv, og, ch, cb, cc, ga, gb = jnp.split(h @ w_in, idx, axis=-1)
    f = lb + (1.0 - lb) * jax.nn.sigmoid(fpre.astype(f32))
    logf = jnp.log(f)
    k = 1.0 - f
    heads = lambda t: t.reshape(bsz, L, HG_HEADS, t.shape[-1] // HG_HEADS)
    o, S_fin = hgrn2_recurrence(heads(q.astype(f32)), heads(k), heads(iv.astype(f32)), heads(logf), S0.astype(f32))
    o = rmsnorm(o, hg_norm.reshape(HG_HEADS, HG_DV)).reshape(bsz, L, HG_WIDTH).astype(h.dtype)
    a = (o * jax.nn.silu(og)) @ w_a
    u = cc * ch
    up = jnp.concatenate([conv_buf.astype(u.dtype), u], axis=1)
    z = sum(conv_w[j] * up[:, j:j + L] for j in range(CONV_K))
    bb = (cb * z) @ w_b
    merged = jax.nn.sigmoid(ga) * a + jax.nn.sigmoid(gb) * bb
    new_buf = up[:, up.shape[1] - (CONV_K - 1):]
    return merged @ w_o, S_fin, new_buf


def memory_kv(mem, g_mem, w_xk, w_xv):
    bsz = mem.shape[0]
    m = rmsnorm(mem, g_mem)
    mk = (m @ w_xk).reshape(bsz, N_MEM, XA_HEADS, XA_HEAD_DIM)
    mv = (m @ w_xv).reshape(bsz, N_MEM, XA_HEADS, XA_HEAD_DIM)
    return mk, mv


def cross_attention(h, mk, mv, w_xq, w_xo):
    bsz, L, _ = h.shape
    q = (h @ w_xq).reshape(bsz, L, XA_HEADS, XA_HEAD_DIM)
    s = jnp.einsum('blhd,bmhd->bhlm', q, mk.astype(q.dtype)).astype(jnp.float32) * (XA_HEAD_DIM ** -0.5)
    p = jax.nn.softmax(s, axis=-1).astype(h.dtype)
    o = jnp.einsum('bhlm,bmhd->blhd', p, mv.astype(h.dtype)).reshape(bsz, L, D_MODEL)
    return o @ w_xo


def swiglu(h, w_gate, w_up, w_down):
    return (jax.nn.silu(h @ w_gate) * (h @ w_up)) @ w_down


def layer(x, g_mix, w_in, lb, hg_norm, conv_w, w_a, w_b, w_o, g_xa, w_xq, w_xo, g_ffn, w_gate, w_up, w_down, S0, conv_buf, mk, mv):
    m, S_fin, new_buf = gated_mixer(rmsnorm(x, g_mix), w_in, lb, hg_norm, conv_w, w_a, w_b, w_o, S0, conv_buf)
    x = x + m
    x = x + cross_attention(rmsnorm(x, g_xa), mk, mv, w_xq, w_xo)
    x = x + swiglu(rmsnorm(x, g_ffn), w_gate, w_up, w_down)
    return x, S_fin, new_buf


def setup_inputs(seed: int = 0) -> dict:
    key = jax.random.key(seed)
    ks = jax.random.split(key, 32)
    f32 = jnp.float32
    nrm = lambda k, shape, scale: jax.random.normal(k, shape, f32) * scale
    gain = lambda k, shape: 1.0 + 0.02 * jax.random.normal(k, shape, f32)
    return {
        'x_prompt': nrm(ks[0], (BATCH, SEQ, D_MODEL), 1.0),
        'x_sample': nrm(ks[1], (DEC_BATCH, DEC_SEQ, D_MODEL), 1.0),
        'cache_mem_k': nrm(ks[2], (DEPTH, DEC_BATCH, N_MEM, XA_HEADS, XA_HEAD_DIM), 1.0),
        'cache_mem_v': nrm(ks[3], (DEPTH, DEC_BATCH, N_MEM, XA_HEADS, XA_HEAD_DIM), 1.0),
        'state_hgrn': nrm(ks[4], (DEPTH, DEC_BATCH, HG_HEADS, HG_DK, HG_DV), 1.0),
        'state_conv': nrm(ks[5], (DEPTH, DEC_BATCH, CONV_K - 1, CONV_WIDTH), 0.5),
        'mem_prompt': nrm(ks[6], (BATCH, N_MEM, D_MODEL), 1.0),
        'norm_mix': gain(ks[7], (DEPTH, D_MODEL)),
        'w_in': nrm(ks[8], (DEPTH, D_MODEL, PROJ_WIDTH), D_MODEL ** -0.5),
        'lb_logits': nrm(ks[9], (DEPTH + 1, HG_WIDTH), 0.5),
        'hg_norm': gain(ks[10], (DEPTH, HG_WIDTH)),
        'conv_w': nrm(ks[11], (DEPTH, CONV_K, CONV_WIDTH), CONV_K ** -0.5),
        'w_a': nrm(ks[12], (DEPTH, HG_WIDTH, D_MODEL), HG_WIDTH ** -0.5),
        'w_b': nrm(ks[13], (DEPTH, CONV_WIDTH, D_MODEL), CONV_WIDTH ** -0.5),
        'w_o': nrm(ks[14], (DEPTH, D_MODEL, D_MODEL), D_MODEL ** -0.5),
        'norm_xattn': gain(ks[15], (DEPTH, D_MODEL)),
        'norm_mem': gain(ks[16], (DEPTH, D_MODEL)),
        'w_xq': nrm(ks[17], (DEPTH, D_MODEL, D_MODEL), D_MODEL ** -0.5),
        'w_xk': nrm(ks[18], (DEPTH, D_MODEL, D_MODEL), D_MODEL ** -0.5),
        'w_xv': nrm(ks[19], (DEPTH, D_MODEL, D_MODEL), D_MODEL ** -0.5),
        'w_xo': nrm(ks[20], (DEPTH, D_MODEL, D_MODEL), D_MODEL ** -0.5),
        'norm_ffn': gain(ks[21], (DEPTH, D_MODEL)),
        'w_gate': nrm(ks[22], (DEPTH, D_MODEL, FFN_HIDDEN), D_MODEL ** -0.5),
        'w_up': nrm(ks[23], (DEPTH, D_MODEL, FFN_HIDDEN), D_MODEL ** -0.5),
        'w_down': nrm(ks[24], (DEPTH, FFN_HIDDEN, D_MODEL), FFN_HIDDEN ** -0.5),
        'norm_final': gain(ks[25], (D_MODEL,)),
    }


def reference(x_prompt, x_sample, cache_mem_k, cache_mem_v, state_hgrn, state_conv, mem_prompt,
              norm_mix, w_in, lb_logits, hg_norm, conv_w, w_a, w_b, w_o,
              norm_xattn, norm_mem, w_xq, w_xk, w_xv, w_xo,
              norm_ffn, w_gate, w_up, w_down, norm_final):
    f32 = jnp.float32
    lb_all = jnp.cumsum(jax.nn.softmax(lb_logits.astype(f32), axis=0), axis=0)
    bp = x_prompt.shape[0]
    xp, xs = x_prompt, x_sample
    mk_list, mv_list, sp_list, cp_list, ss_list, cs_list = [], [], [], [], [], []
    for l in range(DEPTH):
        shared = (norm_mix[l], w_in[l], lb_all[l], hg_norm[l], conv_w[l], w_a[l], w_b[l], w_o[l],
                  norm_xattn[l], w_xq[l], w_xo[l], norm_ffn[l], w_gate[l], w_up[l], w_down[l])
        mk_p, mv_p = memory_kv(mem_prompt, norm_mem[l], w_xk[l], w_xv[l])
        S0 = jnp.zeros((bp, HG_HEADS, HG_DK, HG_DV), f32)
        buf0 = jnp.zeros((bp, CONV_K - 1, CONV_WIDTH), xp.dtype)
        xp, S_p, buf_p = layer(xp, *shared, S0, buf0, mk_p, mv_p)
        xs, S_s, buf_s = layer(xs, *shared, state_hgrn[l], state_conv[l], cache_mem_k[l], cache_mem_v[l])
        mk_list.append(mk_p)
        mv_list.append(mv_p)
        sp_list.append(S_p)
        cp_list.append(buf_p)
        ss_list.append(S_s)
        cs_list.append(buf_s)
    y_prompt = rmsnorm(xp, norm_final)
    y_sample = rmsnorm(xs, norm_final)
    return (y_prompt, y_sample, jnp.stack(mk_list), jnp.stack(mv_list), jnp.stack(sp_list), jnp.stack(cp_list), jnp.stack(ss_list), jnp.stack(cs_list))
```

```python
import numpy as np
import concourse.bass as bass
import concourse.mybir as mybir
from concourse.bass_utils import run_bass_kernel_spmd

F32 = mybir.dt.float32
BF16 = mybir.dt.bfloat16
AF = mybir.ActivationFunctionType
ALU = mybir.AluOpType
AX = mybir.AxisListType

P = 128
D = 4096
KD = 32
HW = 2048
NH = 16
FF = 11008
KF = 86
NMEM = 256
XH = 4
PROJ = 22528
EPS = 1e-6
T = 256
TS = 32
C_Q, C_F, C_IV, C_OG, C_CH, C_CB, C_CC, C_GA, C_GB = 0, 2048, 4096, 6144, 8192, 10240, 12288, 14336, 18432
NRING = 3
SLABW = 256


class Op:
    __slots__ = ("eng", "fn", "deps", "signal", "ev", "dma_sem", "idx", "inc")

    def __init__(self, eng, fn, dma_sem, inc):
        self.eng = eng
        self.fn = fn
        self.deps = []
        self.signal = False
        self.ev = None
        self.dma_sem = dma_sem
        self.inc = inc


class Sched:
    ENGS = ("pe", "act", "dve", "pool", "sp")

    def __init__(self):
        self.dry = False
        self.reset()

    def reset(self):
        self.ops = {e: [] for e in self.ENGS}
        self.last_writer = {}
        self.readers = {}
        self.n = 0
        self.plan = []
        self.plan_pos = 0
        self.issued = 0

    def op(self, eng, fn, reads=(), writes=(), dma_sem=None, inc=16, after_all=False):
        if self.dry:
            return None
        o = Op(eng, fn, dma_sem, inc)
        o.idx = self.n
        self.n += 1
        psr = [k for k in reads if isinstance(k, tuple) and k[0] == "ps"]
        if psr:
            reads = [k for k in reads if not (isinstance(k, tuple) and k[0] == "ps")]
            writes = list(writes) + psr
        deps = {}
        if after_all:
            for e_ in ("pe", "act", "dve"):
                if self.ops[e_]:
                    w = self.ops[e_][-1]
                    deps[id(w)] = w
        lw = self.last_writer
        rd = self.readers
        for k in reads:
            w = lw.get(k)
            if w is not None:
                deps[id(w)] = w
        for k in writes:
            w = lw.get(k)
            if w is not None:
                deps[id(w)] = w
            for r in rd.get(k, ()):
                deps[id(r)] = r
        best = {}
        out = []
        for d in deps.values():
            if d is o:
                continue
            if d.dma_sem is not None:
                out.append(d)
            else:
                b = best.get(d.eng)
                if b is None or d.idx > b.idx:
                    best[d.eng] = d
        for e, d in best.items():
            if e == "pe" and eng == "pe" and dma_sem is None:
                continue
            out.append(d)
        for d in out:
            d.signal = True
        o.deps = out
        for k in writes:
            lw[k] = o
            rd[k] = []
        for k in reads:
            rd.setdefault(k, []).append(o)
        self.ops[eng].append(o)
        return o

    def fence(self, old_keys, new_keys):
        if self.dry:
            return
        prev = []
        for k in old_keys:
            w = self.last_writer.get(k)
            if w is not None:
                prev.append(w)
            prev.extend(self.readers.get(k, ()))
        for k in new_keys:
            self.last_writer.pop(k, None)
            self.readers[k] = list(prev)

    def emit(self, nc, engines, sems):
        cnt = {}
        for e in self.ENGS:
            c = 0
            for o in self.ops[e]:
                if o.dma_sem is not None:
                    v = cnt.get(o.dma_sem, 0) + o.inc
                    cnt[o.dma_sem] = v
                    o.ev = (o.dma_sem, v)
                elif o.signal:
                    c += 1
                    o.ev = (sems[e], c)

        def run(ename, eng):
            waited = {}
            for o in self.ops[ename]:
                for d in o.deps:
                    s, v = d.ev
                    key = id(s)
                    if waited.get(key, 0) < v:
                        eng.wait_ge(s, v)
                        waited[key] = v
                ins = o.fn(eng)
                if o.dma_sem is not None:
                    ins.then_inc(o.dma_sem, o.inc)
                elif o.signal:
                    ins.then_inc(sems[ename], 1)
        return run


class Builder:
    def __init__(self, NT, NP1, debug=False, stages=None):
        self.stages = stages if stages is not None else {"p1", "memkv", "tile", "sample", "mixer", "xattn", "ffn"}
        self.NT = NT
        self.NP1 = NP1
        self.SEG = NT * T
        self.debug = debug
        nc = bass.Bass("TRN2", target_bir_lowering=False)
        self.nc = nc
        self.s = Sched()
        self._declare_dram()
        self._alloc()

    def _declare_dram(self):
        nc = self.nc
        di = lambda n, sh: nc.dram_tensor(n, sh, F32, kind="ExternalInput").ap()
        do = lambda n, sh: nc.dram_tensor(n, sh, F32, kind="ExternalOutput").ap()
        SEG = self.SEG
        self.d_xp = di("xp", [SEG, D])
        self.d_xpred = di("xpred", [max(self.NP1, 1) * T, D])
        self.d_xprev = di("xprev", [2, D])
        self.d_xs = di("xs", [TS, D])
        self.d_cmk = di("cmk", [NMEM, D])
        self.d_cmv = di("cmv", [NMEM, D])
        self.d_sh = di("sh", [P, NH * 128])
        self.d_sc = di("sc", [P, 32])
        self.d_memp = di("memp", [NMEM, D])
        self.d_ident = di("ident", [P, P])
        self.d_mask = di("maskc", [64, 64])
        self.d_gains = di("gains", [P, 5 * KD])
        self.d_lbl = di("lbl", [P, 2 * NH])
        self.d_hgn = di("hgn", [P, NH])
        self.d_cw = di("cw", [P, 3 * NH])
        self.w = {}
        self.wshapes = dict((("w_in", [D, PROJ]), ("w_a", [HW, D]), ("w_b", [HW, D]), ("w_o", [D, D]),
                             ("w_xq", [D, D]), ("w_xk", [D, D]), ("w_xv", [D, D]), ("w_xo", [D, D]),
                             ("w_gate", [D, FF]), ("w_up", [D, FF]), ("w_down", [FF, D])))
        self.o_yp = do("yp", [SEG, D])
        self.o_ys = do("ys", [TS, D])
        self.o_mk = do("mk", [NMEM, D])
        self.o_mv = do("mv", [NMEM, D])
        self.o_hp = do("hp", [P, NH * 128])
        self.o_cp = do("cp", [P, 32])
        self.o_hs = do("hs", [P, NH * 128])
        self.o_cs = do("cs", [P, 32])
        if self.debug:
            self.o_dbg = do("dbg", [P, 4 * KD * T])

    def _alloc(self):
        nc = self.nc
        ARENA = 212800 // 4
        arena = nc.alloc_sbuf_tensor("arena", [P, ARENA], F32)
        self.A = arena.ap()
        self.Ab = self.A.bitcast(BF16)
        self.off = 0

        def f32(nbytes_elems, shape=None):
            n = nbytes_elems
            a = self.off // 4
            self.off += n * 4
            ap = self.A[:, a:a + n]
            return ap

        def bf(n):
            n2 = (n + 1) // 2 * 2
            a = self.off // 2
            self.off += n2 * 2
            return self.Ab[:, a:a + n]

        self.f32 = f32
        self.bf = bf
        r3 = lambda ap, k: ap.rearrange("p (k t) -> p k t", k=k)
        self.xT = r3(f32(KD * T), KD)
        self.hT = r3(bf(KD * T), KD)
        self.S = r3(f32(NH * 128), NH)
        self.Sb = r3(bf(NH * 128), NH)
        self.KT = r3(bf(KD * NMEM), KD)
        self.V = r3(bf(2 * D), 2)
        self.ring = [r3(bf(KD * SLABW), KD) for _ in range(NRING)]
        self.ident_f = f32(128)
        self.ident_b = bf(128)
        self.ones_b = bf(128)
        self.mask = f32(64)
        self.gains = r3(f32(5 * KD), 5)
        self.lbl = r3(f32(2 * NH), 2)
        self.lb = f32(NH)
        self.oml = f32(NH)
        self.hgn = f32(NH)
        self.cw = r3(f32(3 * NH), 3)
        self.rmask = f32(2 * T)
        self.rmask32 = f32(2 * TS)
        self.carry = r3(f32(NH * 2), NH)
        self.carry_s = r3(f32(NH * 2), NH)
        self.chp = r3(f32(4), 2)
        self.xprevT = r3(f32(KD * 2), KD)
        self.hprevT = r3(bf(KD * 2), KD)
        self.tiny = f32(16)
        self.sq = [r3(bf(4 * T), 4) for _ in range(2)]
        self.r0 = f32(2 * T)
        self.R = f32(2 * T)
        self.scratch0 = self.off
        self.scratch_bytes = ARENA * 4 - self.off
        assert self.scratch_bytes >= 53700, self.scratch_bytes
        self.psb = []
        for i in range(8):
            if i == 4:
                self.pbf = nc.alloc_psum_tensor("pbf", [P, 1024], BF16).ap()
                self.psb.append(None)
            else:
                self.psb.append(nc.alloc_psum_tensor(f"psb{i}", [P, 512], F32).ap())
        sem = lambda n: nc.alloc_semaphore(n)
        self.sems = {e: sem("e_" + e) for e in Sched.ENGS}
        self.sem_ring = [sem(f"ring{i}") for i in range(NRING)]
        self.sem_xst = [sem(f"xst{i}") for i in range(2)]
        self.sem_kst = [sem(f"kst{i}") for i in range(2)]
        self.sem_c = [sem(f"c{i}") for i in range(6)]
        self.sem_misc = [sem(f"m{i}") for i in range(8)]

    def scratch(self):
        self.off = self.scratch0

    def next_slab(self, wname, k0, nk, c0, ncols):
        s = self.s
        spec = (wname, k0, nk, c0, ncols)
        if s.dry:
            if wname not in self.w:
                self.w[wname] = self.nc.dram_tensor(wname, self.wshapes[wname], F32, kind="ExternalInput").ap()
            s.plan.append(spec)
            return 0
        i = s.plan_pos
        assert s.plan[i] == spec, (s.plan[i], spec)
        s.plan_pos += 1
        while s.issued < len(s.plan) and s.issued <= i + NRING - 1:
            self._issue_slab(s.issued)
            s.issued += 1
        return i % NRING

    def _issue_slab(self, i):
        wname, k0, nk, c0, ncols = self.s.plan[i]
        slot = i % NRING
        dst = self.ring[slot][:, 0:nk, 0:ncols]
        src = self.w[wname][k0 * P:(k0 + nk) * P, c0:c0 + ncols].rearrange("(k p) n -> p k n", p=P)
        self.s.op("pool", lambda e, d=dst, s_=src: e.dma_start(out=d, in_=s_),
                  writes=[("ring", slot)], dma_sem=self.sem_ring[slot])

    def evac_engine(self):
        self._alt = 1 - getattr(self, "_alt", 0)
        return "act" if self._alt else "dve"

    def copy(self, eng, out, in_, reads, writes):
        if eng == "act":
            self.s.op("act", lambda e: e.activation(out=out, in_=in_, func=AF.Copy), reads=reads, writes=writes)
        else:
            self.s.op("dve", lambda e: e.tensor_copy(out=out, in_=in_), reads=reads, writes=writes)

    def ps_acc(self, i):
        b = i % 4
        return self.psb[b].rearrange("p (j t) -> p j t", j=2), ("ps", b)

    def load_xT(self, src_rows, n, xT, xkey):
        s = self.s
        self.scratch()
        st = [self.f32(D) for _ in range(2)]
        nblk = (n + P - 1) // P
        for blk in range(nblk):
            rows = min(P, n - blk * P)
            stg = st[blk % 2]
            skey = ("xst", blk % 2)
            src = src_rows[blk * P:blk * P + rows, :]
            s.op("sp", lambda e, d=stg[0:rows, :], s_=src: e.dma_start(out=d, in_=s_),
                 writes=[skey], dma_sem=self.sem_xst[blk % 2], after_all=True)
            for cg in range(8):
                bank = 5 + (cg % 2)
                tp = self.psb[bank].rearrange("p (i t) -> p i t", i=4)
                for i in range(4):
                    c = 4 * cg + i
                    s.op("pe", lambda e, o=tp[:, i, 0:rows], a=stg[0:rows, c * P:(c + 1) * P], r=rows:
                         e.transpose(out=o, in_=a, identity=self.ident_f[0:r, 0:r]),
                         reads=[skey, "consts"], writes=[("ps", bank)])
                self.copy(self.evac_engine(), xT[:, 4 * cg:4 * cg + 4, blk * P:blk * P + rows], tp[:, :, 0:rows],
                          reads=[("ps", bank)], writes=[xkey])

    def norm(self, xT, xkey, gidx, n, out, okey, out_scale=None):
        s = self.s
        ssq = self.psb[6][:, 0:n]
        for grp in range(8):
            sq = self.sq[grp % 2]
            s.op("act", lambda e, o=sq[:, :, 0:n], a=xT[:, 4 * grp:4 * grp + 4, 0:n]:
                 e.activation(out=o, in_=a, func=AF.Square), reads=[xkey], writes=[("sq", grp % 2)])
            for i in range(4):
                s.op("pe", lambda e, a=sq[:, i, 0:n], st=(grp == 0 and i == 0), sp=(grp == 7 and i == 3):
                     e.matmul(ssq, lhsT=self.ones_b, rhs=a, start=st, stop=sp),
                     reads=[("sq", grp % 2), "consts"], writes=[("ps", 6)])
        r0 = self.r0[:, 0:n]
        R = self.R[:, 0:n]
        s.op("act", lambda e: e.activation(out=r0, in_=ssq, func=AF.Sqrt, bias=EPS, scale=1.0 / D),
             reads=[("ps", 6)], writes=["r0"])
        s.op("dve", lambda e: e.reciprocal(out=R, in_=r0), reads=["r0"], writes=["R"])
        for k in range(KD):
            s.op("dve", lambda e, o=out[:, k, 0:n], a=xT[:, k, 0:n], g=self.gains[:, gidx, k:k + 1]:
                 e.scalar_tensor_tensor(out=o, in0=a, scalar=g, in1=R, op0=ALU.mult, op1=ALU.mult),
                 reads=[xkey, "R", "consts"], writes=[okey])

    def proj(self, wname, k0, nk, c0, ncols, rhs_list, evac):
        s = self.s
        pieces = []
        kk = 0
        while kk < nk:
            m = min(KD, nk - kk)
            pieces.append((kk, m))
            kk += m
        for cs in range(0, ncols, SLABW):
            w_ = min(SLABW, ncols - cs)
            nj = w_ // P
            self._acc_i = getattr(self, "_acc_i", 0) + 1
            acc, pskey = self.ps_acc(self._acc_i)
            accs = [(acc[:, j, :], pskey) for j in range(2)]
            if len(pieces) > 1:
                self._acc_i += 1
                acc2, pskey2 = self.ps_acc(self._acc_i)
                accs = [(acc[:, 0, :], pskey), (acc2[:, 0, :], pskey2)]
            for pi, (pk, pm) in enumerate(pieces):
                slot = self.next_slab(wname, k0 + pk, pm, c0 + cs, w_)
                slab = self.ring[slot]
                for j in range(nj):
                    for k in range(pm):
                        first = (pi == 0 and k == 0)
                        last = (pi == len(pieces) - 1 and k == pm - 1)
                        for g, (rhs_fn, n, rkeys) in enumerate(rhs_list):
                            if g == 0:
                                o = accs[j][0][:, 0:n]
                                wk = [accs[j][1]]
                            else:
                                o = self.psb[7][:, (j * 4 + g) * 32:(j * 4 + g) * 32 + n]
                                wk = [("ps", 7)]
                            s.op("pe", lambda e, o=o, l=slab[:, k, j * P:(j + 1) * P], r=rhs_fn(pk + k), st=first, sp=last:
                                 e.matmul(o, lhsT=l, rhs=r, start=st, stop=sp),
                                 reads=[("ring", slot)] + list(rkeys), writes=wk)
            for j in range(nj):
                oc = (c0 + cs) // P + j
                for g, (rhs_fn, n, rkeys) in enumerate(rhs_list):
                    if g == 0:
                        evac(g, oc, j, accs[j][0][:, 0:n], accs[j][1])
                    else:
                        evac(g, oc, j, self.psb[7][:, (j * 4 + g) * 32:(j * 4 + g) * 32 + n], ("ps", 7))

    def hgrn_alloc(self, n, C):
        nch = n // C
        r2 = lambda ap: ap.rearrange("p (j t) -> p j t", j=2)
        h = {}
        h["Q"] = r2(self.f32(2 * n)); h["G"] = r2(self.f32(2 * n)); h["KK"] = r2(self.f32(2 * n)); h["D1"] = r2(self.f32(2 * n))
        h["B"] = r2(self.f32(2 * n))
        for nm in ("VT", "OGS", "QT", "KT", "KH", "SQ"):
            h[nm] = r2(self.bf(2 * n))
        h["ktv"] = self.bf(nch * 4 * 128).rearrange("p (c i v) -> p c i v", c=nch, i=4)
        h["scm"] = r2(self.bf(2 * C))
        h["DD"] = self.f32(2 * nch).rearrange("p (j c) -> p j c", j=2)
        h["bsum"] = self.f32(2)
        return h

    def hgrn_group(self, hg, n, C, h, hT, state_only, oN=None):
        s = self.s
        nch = n // C
        h0 = 2 * hg
        rhs = [(lambda k: hT[:, k, 0:n], n, ["hT"])]
        Q, G, KK, D1, B = h["Q"], h["G"], h["KK"], h["D1"], h["B"]
        VT, OGS, QT, KT, KH, SQ = h["VT"], h["OGS"], h["QT"], h["KT"], h["KH"], h["SQ"]
        ktv, scm, DD = h["ktv"], h["scm"], h["DD"]

        def ev_act(dst, key, func):
            def f(g, oc, j, ps, pskey):
                s.op("act", lambda e: e.activation(out=dst[:, j, :], in_=ps, func=func), reads=[pskey], writes=[key])
            return f

        def ev_dve(dst, key):
            def f(g, oc, j, ps, pskey):
                s.op("dve", lambda e: e.tensor_copy(out=dst[:, j, :], in_=ps), reads=[pskey], writes=[key])
            return f

        if not state_only:
            self.proj("w_in", 0, KD, C_Q + hg * 256, 256, rhs, ev_act(Q, "hQ", AF.Copy))
        self.proj("w_in", 0, KD, C_F + hg * 256, 256, rhs, ev_act(G, "hG", AF.Sigmoid))
        self.proj("w_in", 0, KD, C_IV + hg * 256, 256, rhs, ev_dve(VT, "hVT"))
        if not state_only:
            self.proj("w_in", 0, KD, C_OG + hg * 256, 256, rhs, ev_act(OGS, "hOGS", AF.Silu))
        for j in range(2):
            hh = h0 + j
            s.op("dve", lambda e, a=G[:, j, :], m=self.oml[:, hh:hh + 1], b=self.lb[:, hh:hh + 1]:
                 e.tensor_scalar(out=a, in0=a, scalar1=m, scalar2=b, op0=ALU.mult, op1=ALU.add),
                 reads=["hG", "consts"], writes=["hG"])
        s.op("dve", lambda e: e.tensor_scalar(out=KK, in0=G, scalar1=-1.0, scalar2=1.0, op0=ALU.mult, op1=ALU.add),
             reads=["hG"], writes=["hKK"])
        s.op("act", lambda e: e.activation(out=G, in_=G, func=AF.Ln), reads=["hG"], writes=["hG"])
        Gf = G.rearrange("p j t -> p (j t)")
        Bf = B.rearrange("p j t -> p (j t)")
        rm = self.rmask_C[C][:, 0:2 * n]
        s.op("dve", lambda e: e.tensor_tensor_scan(out=Bf, data0=rm, data1=Gf, initial=0.0, op0=ALU.mult, op1=ALU.add),
             reads=["hG", "consts"], writes=["hB"])
        B4 = B.rearrange("p j (c t) -> p j c t", c=nch)
        bl = B4[:, :, :, C - 1:C]
        if not state_only:
            s.op("act", lambda e: e.activation(out=D1, in_=B, func=AF.Exp), reads=["hB"], writes=["hD1"])
            s.op("dve", lambda e: e.tensor_tensor(out=QT, in0=Q, in1=D1, op=ALU.mult), reads=["hQ", "hD1"], writes=["hQT"])
            s.op("act", lambda e: e.activation(out=D1, in_=B, func=AF.Exp, scale=-1.0), reads=["hB", "hQT"], writes=["hD1"])
            s.op("dve", lambda e: e.tensor_tensor(out=KT, in0=KK, in1=D1, op=ALU.mult), reads=["hKK", "hD1"], writes=["hKT"])
        D14 = D1.rearrange("p j (c t) -> p j c t", c=nch)
        s.op("dve", lambda e: e.tensor_tensor(out=D14, in0=bl.broadcast_to([P, 2, nch, C]), in1=B4, op=ALU.subtract),
             reads=["hB", "hKT"], writes=["hD1"])
        s.op("act", lambda e: e.activation(out=D1, in_=D1, func=AF.Exp), reads=["hD1"], writes=["hD1"])
        s.op("dve", lambda e: e.tensor_tensor(out=KH, in0=KK, in1=D1, op=ALU.mult), reads=["hKK", "hD1"], writes=["hKH"])
        s.op("act", lambda e: e.activation(out=DD, in_=B4[:, :, :, C - 1], func=AF.Exp),
             reads=["hB"], writes=["hDD"])
        tpk = self.pbf[:, 0:512].rearrange("p (i v) -> p i v", i=4)
        for c in range(nch):
            for j in range(2):
                s.op("pe", lambda e, o=tpk[0:C, j, :], a=KH[:, j, c * C:(c + 1) * C]:
                     e.transpose(out=o, in_=a, identity=self.ident_b), reads=["hKH", "consts"], writes=[("ps", 4)])
                s.op("pe", lambda e, o=tpk[0:C, 2 + j, :], a=VT[:, j, c * C:(c + 1) * C]:
                     e.transpose(out=o, in_=a, identity=self.ident_b), reads=["hVT", "consts"], writes=[("ps", 4)])
            self.copy(self.evac_engine(), ktv[0:C, c, :, :], tpk[0:C, :, :], reads=[("ps", 4)], writes=["hktv"])
        OT = Q
        sc = self.psb[5][:, 0:2 * C].rearrange("p (j t) -> p j t", j=2)
        po = self.psb[6][:, 0:2 * C].rearrange("p (j t) -> p j t", j=2)
        pst = self.psb[7][:, 0:256].rearrange("p (j v) -> p j v", j=2)
        for c in range(nch):
            cs = slice(c * C, (c + 1) * C)
            if not state_only:
                for j in range(2):
                    s.op("pe", lambda e, o=sc[0:C, j, :], l=KT[:, j, cs], r=QT[:, j, cs]:
                         e.matmul(o, lhsT=l, rhs=r, start=True, stop=True), reads=["hKT", "hQT"], writes=[("ps", 5)])
                s.op("dve", lambda e: e.tensor_tensor(out=scm[0:C, :, :], in0=sc[0:C, :, :],
                                                      in1=self.mask_C[C].unsqueeze(1).broadcast_to([C, 2, C]), op=ALU.mult),
                     reads=[("ps", 5), "consts"], writes=["hscm"])
                for j in range(2):
                    s.op("pe", lambda e, o=po[:, j, :], l=self.Sb[:, h0 + j, :], r=QT[:, j, cs]:
                         e.matmul(o, lhsT=l, rhs=r, start=True, stop=False), reads=["Sb", "hQT"], writes=[("ps", 6)])
                    s.op("pe", lambda e, o=po[:, j, :], l=ktv[0:C, c, 2 + j, :], r=scm[0:C, j, :]:
                         e.matmul(o, lhsT=l, rhs=r, start=False, stop=True), reads=["hktv", "hscm"], writes=[("ps", 6)])
                s.op("act", lambda e, o=OT[:, :, cs]: e.activation(out=o, in_=po, func=AF.Copy),
                     reads=[("ps", 6), "hQT"], writes=["hQ"])
            for j in range(2):
                s.op("pe", lambda e, o=pst[:, j, :], l=ktv[0:C, c, j, :], r=ktv[0:C, c, 2 + j, :]:
                     e.matmul(o, lhsT=l, rhs=r, start=True, stop=True), reads=["hktv"], writes=[("ps", 7)])
            for j in range(2):
                s.op("dve", lambda e, a=self.S[:, h0 + j, :], d=DD[:, j, c:c + 1], p_=pst[:, j, :]:
                     e.scalar_tensor_tensor(out=a, in0=a, scalar=d, in1=p_, op0=ALU.mult, op1=ALU.add),
                     reads=[("ps", 7), "hDD", "S"], writes=["S"])
            if not state_only:
                s.op("act", lambda e: e.activation(out=self.Sb[:, h0:h0 + 2, :], in_=self.S[:, h0:h0 + 2, :], func=AF.Copy),
                     reads=["S"], writes=["Sb"])
        if state_only:
            return
        s.op("act", lambda e: e.activation(out=SQ, in_=OT, func=AF.Square), reads=["hQ"], writes=["hSQ"])
        ssq = self.psb[5][:, 0:2 * n].rearrange("p (j t) -> p j t", j=2)
        for j in range(2):
            s.op("pe", lambda e, o=ssq[:, j, :], a=SQ[:, j, :]: e.matmul(o, lhsT=self.ones_b, rhs=a, start=True, stop=True),
                 reads=["hSQ", "consts"], writes=[("ps", 5)])
        s.op("act", lambda e: e.activation(out=D1, in_=ssq, func=AF.Sqrt, bias=EPS, scale=1.0 / 128),
             reads=[("ps", 5)], writes=["hD1"])
        s.op("dve", lambda e: e.reciprocal(out=D1, in_=D1), reads=["hD1"], writes=["hD1"])
        for j in range(2):
            hh = h0 + j
            s.op("dve", lambda e, a=OT[:, j, :], g=self.hgn[:, hh:hh + 1], r=D1[:, j, :]:
                 e.scalar_tensor_tensor(out=a, in0=a, scalar=g, in1=r, op0=ALU.mult, op1=ALU.mult),
                 reads=["hQ", "hD1", "consts"], writes=["hQ"])
        s.op("dve", lambda e: e.tensor_tensor(out=oN[:, h0:h0 + 2, 0:n], in0=OT, in1=OGS, op=ALU.mult),
             reads=["hQ", "hOGS"], writes=["oN"])

    def mixer(self, n, C, carry, ckey, with_prev):
        s = self.s
        xT, hT = self.xT, self.hT
        self.norm(xT, "xT", 0, n, hT, "hT")
        if with_prev:
            self.norm(self.xprevT, "xprevT", 0, 2, self.hprevT, "hprevT")
        self.scratch()
        r3 = lambda ap, k: ap.rearrange("p (k t) -> p k t", k=k)
        oN = r3(self.bf(NH * n), NH)
        cbz = r3(self.bf(NH * n), NH)
        merged = r3(self.bf(KD * n), KD)
        grp0 = self.off
        h = self.hgrn_alloc(n, C)
        s.fence(["mSGA", "mSGB", "mTA", "mTB", "cCH", "cU"], ["hQ", "hG", "hKK", "hD1", "hB", "hVT", "hOGS", "hQT", "hKT", "hKH", "hSQ", "hktv", "hscm", "hDD"])
        for hg in range(NH // 2):
            self.hgrn_group(hg, n, C, h, hT, False, oN)
        self.off = grp0
        s.fence(["hQ", "hG", "hKK", "hD1", "hB", "hVT", "hOGS", "hQT", "hKT", "hKH", "hSQ", "hktv", "hscm", "hDD"], ["cCH", "cU"])
        r2 = lambda ap: ap.rearrange("p (j t) -> p j t", j=2)
        CH = r2(self.f32(2 * n))
        U = r2(self.f32(2 * (n + 2)))
        rhs = [(lambda k: hT[:, k, 0:n], n, ["hT"])]
        if with_prev:
            rhs.append((lambda k: self.hprevT[:, k, 0:2], 2, ["hprevT"]))
        for cp in range(NH // 2):
            def ev_ch(g, oc, j, ps, pskey):
                if g == 0:
                    s.op("act", lambda e: e.activation(out=CH[:, j, :], in_=ps, func=AF.Copy), reads=[pskey], writes=["cCH"])
                else:
                    s.op("act", lambda e: e.activation(out=self.chp[:, j, :], in_=ps, func=AF.Copy), reads=[pskey], writes=["chp"])

            def ev_cc(g, oc, j, ps, pskey):
                cidx = 2 * cp + j
                if g == 0:
                    s.op("dve", lambda e: e.tensor_tensor(out=U[:, j, 2:n + 2], in0=ps, in1=CH[:, j, :], op=ALU.mult),
                         reads=[pskey, "cCH"], writes=["cU"])
                else:
                    s.op("dve", lambda e: e.tensor_tensor(out=carry[:, cidx, :], in0=ps, in1=self.chp[:, j, :], op=ALU.mult),
                         reads=[pskey, "chp"], writes=[ckey])

            def ev_cb(g, oc, j, ps, pskey):
                cidx = 2 * cp + j
                s.op("dve", lambda e: e.tensor_tensor(out=cbz[:, cidx, 0:n], in0=ps, in1=CH[:, j, :], op=ALU.mult),
                     reads=[pskey, "cCH"], writes=["cbz"])

            self.proj("w_in", 0, KD, C_CH + cp * 256, 256, rhs, ev_ch)
            self.proj("w_in", 0, KD, C_CC + cp * 256, 256, rhs, ev_cc)
            for j in range(2):
                cidx = 2 * cp + j
                s.op("dve", lambda e, j=j, cidx=cidx: e.tensor_copy(out=U[:, j, 0:2], in_=carry[:, cidx, :]),
                     reads=[ckey], writes=["cU"])
                for tap in range(3):
                    wv = self.cw[:, tap, cidx:cidx + 1]
                    if tap == 0:
                        s.op("dve", lambda e, j=j, wv=wv: e.tensor_scalar(out=CH[:, j, :], in0=U[:, j, 0:n], scalar1=wv,
                                                                          scalar2=None, op0=ALU.mult),
                             reads=["cU", "consts"], writes=["cCH"])
                    else:
                        s.op("dve", lambda e, j=j, wv=wv, tap=tap: e.scalar_tensor_tensor(
                            out=CH[:, j, :], in0=U[:, j, tap:tap + n], scalar=wv, in1=CH[:, j, :], op0=ALU.mult, op1=ALU.add),
                             reads=["cU", "cCH", "consts"], writes=["cCH"])
                s.op("act", lambda e, j=j, cidx=cidx: e.activation(out=carry[:, cidx, :], in_=U[:, j, n:n + 2], func=AF.Copy),
                     reads=["cU"], writes=[ckey])
            self.proj("w_in", 0, KD, C_CB + cp * 256, 256, rhs[:1], ev_cb)
        self.off = grp0
        s.fence(["cCH", "cU"], ["mSGA", "mSGB", "mTA", "mTB"])
        SGA = r2(self.f32(2 * n)); SGB = r2(self.f32(2 * n)); TA = r2(self.f32(2 * n)); TB = r2(self.f32(2 * n))
        rhs_h = [(lambda k: hT[:, k, 0:n], n, ["hT"])]
        rhs_o = [(lambda k: oN[:, k, 0:n], n, ["oN"])]
        rhs_c = [(lambda k: cbz[:, k, 0:n], n, ["cbz"])]
        for jp in range(KD // 2):
            def ev_sig(dst, key):
                def f(g, oc, j, ps, pskey):
                    s.op("act", lambda e: e.activation(out=dst[:, j, :], in_=ps, func=AF.Sigmoid), reads=[pskey], writes=[key])
                return f

            def ev_a(g, oc, j, ps, pskey):
                s.op("dve", lambda e: e.tensor_tensor(out=TA[:, j, :], in0=ps, in1=SGA[:, j, :], op=ALU.mult),
                     reads=[pskey, "mSGA"], writes=["mTA"])

            def ev_b(g, oc, j, ps, pskey):
                s.op("dve", lambda e: e.tensor_tensor(out=TB[:, j, :], in0=ps, in1=SGB[:, j, :], op=ALU.mult),
                     reads=[pskey, "mSGB"], writes=["mTB"])
                mdst = merged[:, oc, 0:n]
                s.op("dve", lambda e: e.tensor_tensor(out=mdst, in0=TA[:, j, :], in1=TB[:, j, :], op=ALU.add),
                     reads=["mTA", "mTB"], writes=["merged"])

            self.proj("w_in", 0, KD, C_GA + jp * 256, 256, rhs_h, ev_sig(SGA, "mSGA"))
            self.proj("w_in", 0, KD, C_GB + jp * 256, 256, rhs_h, ev_sig(SGB, "mSGB"))
            self.proj("w_a", 0, NH, jp * 256, 256, rhs_o, ev_a)
            self.proj("w_b", 0, NH, jp * 256, 256, rhs_c, ev_b)
        self.proj("w_o", 0, KD, 0, D, [(lambda k: merged[:, k, 0:n], n, ["merged"])], self.ev_resid(n))

    def ev_resid(self, n):
        s = self.s

        def f(g, oc, j, ps, pskey):
            s.op("dve", lambda e: e.tensor_tensor(out=self.xT[:, oc, 0:n], in0=self.xT[:, oc, 0:n], in1=ps, op=ALU.add),
                 reads=[pskey, "xT"], writes=["xT"])
        return f

    def xattn(self, n):
        s = self.s
        xT, hT = self.xT, self.hT
        self.norm(xT, "xT", 1, n, hT, "hT")
        self.scratch()
        r3 = lambda ap, k: ap.rearrange("p (k t) -> p k t", k=k)
        qT = r3(self.bf(KD * n), KD)
        aT = r3(self.bf(KD * n), KD)
        Pexp = [self.f32(NMEM) for _ in range(2)]
        Pn = [self.bf(NMEM) for _ in range(2)]
        PTs = [r3(self.bf(2 * n), 2) for _ in range(2)]
        st = [self.f32(4) for _ in range(2)]

        def ev_q(g, oc, j, ps, pskey):
            s.op("act", lambda e: e.activation(out=qT[:, oc, 0:n], in_=ps, func=AF.Copy), reads=[pskey], writes=["qT"])
        self.proj("w_xq", 0, KD, 0, D, [(lambda k: hT[:, k, 0:n], n, ["hT"])], ev_q)
        scale = 1.0 / 32.0
        nblk = (n + P - 1) // P
        it = 0
        pbf = self.pbf.rearrange("p (i t) -> p i t", i=8)
        for hd in range(XH):
            pt = PTs[hd % 2]
            ptkey = ("PTs", hd % 2)
            for tb in range(nblk):
                rows = min(P, n - tb * P)
                bank = 5 + (it % 2)
                sc = self.psb[bank][0:rows, 0:NMEM]
                pe_, pn_, st_ = Pexp[it % 2], Pn[it % 2], st[it % 2]
                kx = it % 2
                for kc in range(8):
                    s.op("pe", lambda e, l=qT[:, 8 * hd + kc, tb * P:tb * P + rows], r=self.KT[:, 8 * hd + kc, :], a=(kc == 0), b=(kc == 7), sc=sc:
                         e.matmul(sc, lhsT=l, rhs=r, start=a, stop=b), reads=["qT", "KT"], writes=[("ps", bank)])
                s.op("dve", lambda e, sc=sc, st_=st_, rows=rows: e.tensor_reduce(out=st_[0:rows, 0:1], in_=sc, axis=AX.X, op=ALU.max),
                     reads=[("ps", bank)], writes=[("ast", kx)])
                s.op("dve", lambda e, st_=st_, rows=rows: e.tensor_scalar(out=st_[0:rows, 1:2], in0=st_[0:rows, 0:1], scalar1=-scale,
                                                                          scalar2=None, op0=ALU.mult),
                     reads=[("ast", kx)], writes=[("ast", kx)])
                s.op("act", lambda e, sc=sc, st_=st_, pe_=pe_, rows=rows: e.activation(
                    out=pe_[0:rows, :], in_=sc, func=AF.Exp, bias=st_[0:rows, 1:2], scale=scale, accum_out=st_[0:rows, 2:3]),
                     reads=[("ps", bank), ("ast", kx)], writes=[("ast", kx), ("Pexp", kx)])
                s.op("dve", lambda e, st_=st_, rows=rows: e.reciprocal(out=st_[0:rows, 3:4], in_=st_[0:rows, 2:3]),
                     reads=[("ast", kx)], writes=[("ast", kx)])
                s.op("dve", lambda e, st_=st_, pe_=pe_, pn_=pn_, rows=rows: e.tensor_scalar(
                    out=pn_[0:rows, :], in0=pe_[0:rows, :], scalar1=st_[0:rows, 3:4], scalar2=None, op0=ALU.mult),
                     reads=[("ast", kx), ("Pexp", kx)], writes=[("Pn", kx)])
                for mh in range(2):
                    s.op("pe", lambda e, o=pbf[:, 4 + (it % 2) * 2 + mh, 0:rows], a=pn_[0:rows, mh * P:(mh + 1) * P], rows=rows:
                         e.transpose(out=o, in_=a, identity=self.ident_b[0:rows, 0:rows]),
                         reads=[("Pn", kx), "consts"], writes=[("ps", 4)])
                s.op("act", lambda e, o=pt[:, :, tb * P:tb * P + rows], a=pbf[:, 4 + (it % 2) * 2:4 + (it % 2) * 2 + 2, 0:rows]:
                     e.activation(out=o, in_=a, func=AF.Copy), reads=[("ps", 4)], writes=[ptkey])
                it += 1
            for dc in range(8):
                half = dc % 2
                pv = self.psb[7][:, half * 256:half * 256 + n]
                for mh in range(2):
                    s.op("pe", lambda e, pv=pv, l=self.V[:, mh, hd * 1024 + dc * P:hd * 1024 + (dc + 1) * P], r=pt[:, mh, 0:n], a=(mh == 0), b=(mh == 1):
                         e.matmul(pv, lhsT=l, rhs=r, start=a, stop=b), reads=["V", ptkey], writes=[("ps", 7)])
                s.op("act", lambda e, pv=pv, o=aT[:, 8 * hd + dc, 0:n]: e.activation(out=o, in_=pv, func=AF.Copy),
                     reads=[("ps", 7)], writes=["aT"])
        self.proj("w_xo", 0, KD, 0, D, [(lambda k: aT[:, k, 0:n], n, ["aT"])], self.ev_resid(n))

    def ffn(self, n):
        s = self.s
        xT, hT = self.xT, self.hT
        self.norm(xT, "xT", 2, n, hT, "hT")
        self.scratch()
        hid = self.bf(KF * n).rearrange("p (k t) -> p k t", k=KF)
        GS = self.f32(2 * n).rearrange("p (j t) -> p j t", j=2)
        rhs = [(lambda k: hT[:, k, 0:n], n, ["hT"])]
        for hp in range(KF // 2):
            def ev_g(g, oc, j, ps, pskey):
                s.op("act", lambda e: e.activation(out=GS[:, j, :], in_=ps, func=AF.Silu), reads=[pskey], writes=["fGS"])

            def ev_u(g, oc, j, ps, pskey):
                s.op("dve", lambda e: e.tensor_tensor(out=hid[:, oc, 0:n], in0=ps, in1=GS[:, j, :], op=ALU.mult),
                     reads=[pskey, "fGS"], writes=["hid"])
            self.proj("w_gate", 0, KD, hp * 256, 256, rhs, ev_g)
            self.proj("w_up", 0, KD, hp * 256, 256, rhs, ev_u)
        self.proj("w_down", 0, KF, 0, D, [(lambda k: hid[:, k, 0:n], n, ["hid"])], self.ev_resid(n))

    def final_out(self, n, dst_rows):
        s = self.s
        xT = self.xT
        self.norm(xT, "xT", 4, n, xT, "xT")
        self.scratch()
        st = [self.f32(D) for _ in range(2)]
        nblk = (n + P - 1) // P
        for blk in range(nblk):
            rows = min(P, n - blk * P)
            stg = st[blk % 2]
            skey = ("xst", blk % 2)
            for cg in range(8):
                bank = 5 + (cg % 2)
                tp = self.psb[bank].rearrange("p (i t) -> p i t", i=4)
                for i in range(4):
                    c = 4 * cg + i
                    s.op("pe", lambda e, o=tp[0:rows, i, :], a=xT[:, c, blk * P:blk * P + rows]:
                         e.transpose(out=o, in_=a, identity=self.ident_f), reads=["xT", "consts"], writes=[("ps", bank)])
                self.copy(self.evac_engine(), stg[0:rows, cg * 512:(cg + 1) * 512].rearrange("p (i t) -> p i t", i=4),
                          tp[0:rows, :, :], reads=[("ps", bank)], writes=[skey])
            s.op("sp", lambda e, d=dst_rows[blk * P:blk * P + rows, :], a=stg[0:rows, :]: e.dma_start(out=d, in_=a),
                 reads=[skey], writes=[("out", self._nout())], dma_sem=self.sem_xst[blk % 2])

    def _nout(self):
        self._outc = getattr(self, "_outc", 0) + 1
        self.outkeys.append(("out", self._outc))
        return self._outc

    def setup(self):
        s = self.s
        nc = self.nc
        ld = lambda dst, src, key="consts": s.op("sp", lambda e: e.dma_start(out=dst, in_=src), writes=[key], dma_sem=self._csem())
        ld(self.ident_f, self.d_ident)
        ld(self.mask[0:64, :], self.d_mask)
        ld(self.gains.rearrange("p g k -> p (g k)"), self.d_gains)
        ld(self.lbl.rearrange("p r h -> p (r h)"), self.d_lbl)
        ld(self.hgn, self.d_hgn)
        ld(self.cw.rearrange("p t c -> p (t c)"), self.d_cw)
        s.op("act", lambda e: e.activation(out=self.ident_b, in_=self.ident_f, func=AF.Copy), reads=["consts"], writes=["consts"])
        s.op("dve", lambda e: e.memset(self.ones_b, 1.0), writes=["consts"])
        s.op("dve", lambda e: e.memset(self.rmask, 1.0), writes=["consts"])
        s.op("dve", lambda e: e.memset(self.rmask.rearrange("p (n c) -> p n c", c=64)[:, :, 0:1], 0.0), writes=["consts"])
        s.op("dve", lambda e: e.memset(self.rmask32, 1.0), writes=["consts"])
        s.op("dve", lambda e: e.memset(self.rmask32.rearrange("p (n c) -> p n c", c=32)[:, :, 0:1], 0.0), writes=["consts"])
        s.op("dve", lambda e: e.tensor_tensor(out=self.lb, in0=self.lbl[:, 0, :], in1=self.lbl[:, 1, :], op=ALU.subtract),
             reads=["consts"], writes=["consts"])
        s.op("act", lambda e: e.activation(out=self.lb, in_=self.lb, func=AF.Sigmoid), reads=["consts"], writes=["consts"])
        s.op("dve", lambda e: e.tensor_scalar(out=self.oml, in0=self.lb, scalar1=-1.0, scalar2=1.0, op0=ALU.mult, op1=ALU.add),
             reads=["consts"], writes=["consts"])
        self.mask_C = {64: self.mask[0:64, 0:64], 32: self.mask[0:32, 0:32]}
        self.rmask_C = {64: self.rmask, 32: self.rmask32}

    def _csem(self):
        self._ci = getattr(self, "_ci", -1) + 1
        return self.sem_c[self._ci % len(self.sem_c)]

    def zero_state(self):
        s = self.s
        s.op("dve", lambda e: e.memset(self.S.rearrange("p h v -> p (h v)"), 0.0), writes=["S"])
        s.op("dve", lambda e: e.memset(self.Sb.rearrange("p h v -> p (h v)"), 0.0), writes=["Sb"])

    def phase1_tile(self, src_rows):
        self.load_xT(src_rows, T, self.xT, "xT")
        self.norm(self.xT, "xT", 0, T, self.hT, "hT")
        self.scratch()
        h = self.hgrn_alloc(T, 64)
        for hg in range(NH // 2):
            self.hgrn_group(hg, T, 64, h, self.hT, True)

    def memkv(self):
        s = self.s
        self.load_xT(self.d_memp, NMEM, self.xT, "xT")
        self.norm(self.xT, "xT", 3, NMEM, self.hT, "hT")
        self.scratch()
        ktok = self.bf(2 * D).rearrange("p (b d) -> p b d", b=2)
        stg = [self.f32(SLABW) for _ in range(2)]
        it = 0
        for wname, dst_bf, okey, dout in (("w_xk", ktok, "ktok", self.o_mk), ("w_xv", self.V, "V", self.o_mv)):
            for cs in range(0, D, SLABW):
                slot = self.next_slab(wname, 0, KD, cs, SLABW)
                slab = self.ring[slot]
                self._acc_i = getattr(self, "_acc_i", 0) + 1
                acc, pskey = self.ps_acc(self._acc_i)
                for mb in range(2):
                    for k in range(KD):
                        s.op("pe", lambda e, o=acc[:, mb, :], l=self.hT[:, k, mb * P:(mb + 1) * P], r=slab[:, k, :], a=(k == 0), b=(k == KD - 1):
                             e.matmul(o, lhsT=l, rhs=r, start=a, stop=b), reads=[("ring", slot), "hT"], writes=[pskey])
                for mb in range(2):
                    sg = stg[it % 2]
                    skey = ("kst", it % 2)
                    s.op("act", lambda e, o=sg, a=acc[:, mb, :]: e.activation(out=o, in_=a, func=AF.Copy), reads=[pskey], writes=[skey])
                    s.op("dve", lambda e, o=dst_bf[:, mb, cs:cs + SLABW], a=acc[:, mb, :]: e.tensor_copy(out=o, in_=a),
                         reads=[pskey], writes=[okey])
                    s.op("sp", lambda e, d=dout[mb * P:(mb + 1) * P, cs:cs + SLABW], a=sg: e.dma_start(out=d, in_=a),
                         reads=[skey, ("xst", 1)], writes=[("out", self._nout())], dma_sem=self.sem_kst[it % 2])
                    it += 1
        self.k_transposes(ktok, "ktok", [("xst", 0)])

    def k_transposes(self, ktok, kkey, extra=()):
        s = self.s
        pbf = self.pbf.rearrange("p (i t) -> p i t", i=8)
        for c in range(KD):
            g = c % 2
            for mb in range(2):
                s.op("pe", lambda e, o=pbf[:, 4 * g + mb, :], a=ktok[:, mb, c * P:(c + 1) * P]:
                     e.transpose(out=o, in_=a, identity=self.ident_b), reads=[kkey, "consts"] + list(extra), writes=[("ps", 4)])
            self.copy(self.evac_engine(), self.KT[:, c, :].rearrange("p (b m) -> p b m", b=2), pbf[:, 4 * g:4 * g + 2, :],
                      reads=[("ps", 4)], writes=["KT"])

    def prompt_tile(self, t):
        n = T
        self.load_xT(self.d_xp[t * T:(t + 1) * T, :], n, self.xT, "xT")
        if t == 0:
            self.load_xT(self.d_xprev, 2, self.xprevT, "xprevT")
        if "mixer" in self.stages:
            self.mixer(n, 64, self.carry, "carry", with_prev=(t == 0))
        if self.debug and t == 0:
            self.dbg_dump(0)
        if "xattn" in self.stages:
            self.xattn(n)
        if self.debug and t == 0:
            self.dbg_dump(1)
        if "ffn" in self.stages:
            self.ffn(n)
        if self.debug and t == 0:
            self.dbg_dump(2)
        self.final_out(n, self.o_yp[t * T:(t + 1) * T, :])

    def dbg_dump(self, i):
        s = self.s
        s.op("sp", lambda e: e.dma_start(out=self.o_dbg[:, i * KD * T:(i + 1) * KD * T], in_=self.xT.rearrange("p k t -> p (k t)")),
             reads=["xT"], writes=[("out", self._nout())], dma_sem=self.sem_misc[i])

    def store_small(self, dst, src, key, semi):
        self.s.op("sp", lambda e: e.dma_start(out=dst, in_=src), reads=[key], writes=[("out", self._nout())], dma_sem=self.sem_misc[semi])

    def sample_pass(self):
        s = self.s
        n = TS
        s.op("sp", lambda e: e.dma_start(out=self.S.rearrange("p h v -> p (h v)"), in_=self.d_sh), writes=["S"], dma_sem=self.sem_misc[4])
        s.op("act", lambda e: e.activation(out=self.Sb.rearrange("p h v -> p (h v)"), in_=self.S.rearrange("p h v -> p (h v)"), func=AF.Copy),
             reads=["S"], writes=["Sb"])
        s.op("sp", lambda e: e.dma_start(out=self.carry_s.rearrange("p c r -> p (c r)"), in_=self.d_sc), writes=["carry_s"], dma_sem=self.sem_misc[5])
        self.load_xT(self.d_xs, n, self.xT, "xT")
        if "mixer" in self.stages:
            self.mixer(n, 32, self.carry_s, "carry_s", with_prev=False)
        self.scratch()
        self.off = self.scratch0 + 20 * 1024
        ktok = self.bf(2 * D).rearrange("p (b d) -> p b d", b=2)
        s.op("pool", lambda e: e.dma_start(out=ktok, in_=self.d_cmk.rearrange("(b p) d -> p b d", p=P)),
             writes=["ktok_s", ("xst", 0), ("xst", 1)], dma_sem=self.sem_misc[6], after_all=True)
        s.op("pool", lambda e: e.dma_start(out=self.V, in_=self.d_cmv.rearrange("(b p) d -> p b d", p=P)),
             writes=["V"], dma_sem=self.sem_misc[7])
        self.k_transposes(ktok, "ktok_s")
        if "xattn" in self.stages:
            self.xattn(n)
        if "ffn" in self.stages:
            self.ffn(n)
        self.final_out(n, self.o_ys)

    def program(self):
        self.outkeys = []
        self._outc = 0
        self._acc_i = 0
        self._alt = 0
        self._ci = -1
        self.setup()
        self.zero_state()
        for t in range(self.NP1 if "p1" in self.stages else 0):
            self.phase1_tile(self.d_xpred[t * T:(t + 1) * T, :])
        s = self.s
        if self.NP1 > 0:
            s.op("act", lambda e: e.activation(out=self.Sb.rearrange("p h v -> p (h v)"), in_=self.S.rearrange("p h v -> p (h v)"), func=AF.Copy),
                 reads=["S"], writes=["Sb"])
        if "memkv" in self.stages:
            self.memkv()
        for t in range(self.NT if "tile" in self.stages else 0):
            self.prompt_tile(t)
        self.store_small(self.o_hp, self.S.rearrange("p h v -> p (h v)"), "S", 0)
        self.store_small(self.o_cp, self.carry.rearrange("p c r -> p (c r)"), "carry", 1)
        if "sample" in self.stages:
            self.sample_pass()
        self.store_small(self.o_hs, self.S.rearrange("p h v -> p (h v)"), "S", 2)
        self.store_small(self.o_cs, self.carry_s.rearrange("p c r -> p (c r)"), "carry_s", 3)
        s.op("sp", lambda e: e.wait_ge(self.sem_misc[0], 0), reads=list(self.outkeys))

    def build(self):
        nc = self.nc
        s = self.s
        s.dry = True
        self.program()
        plan = s.plan
        s.dry = False
        s.reset()
        s.plan = plan
        self.program()
        assert s.plan_pos == len(plan), (s.plan_pos, len(plan))
        run = s.emit(nc, None, self.sems)
        with nc.Block() as block:
            @block.tensor
            def _(e):
                run("pe", e)

            @block.scalar
            def _(e):
                run("act", e)

            @block.vector
            def _(e):
                run("dve", e)

            @block.gpsimd
            def _(e):
                run("pool", e)

            @block.sync
            def _(e):
                run("sp", e)
        return nc


def _pc(v):
    v = np.asarray(v, dtype=np.float32)
    return np.ascontiguousarray(v.reshape(-1, P).T)


def make_in_maps(inp, NT, wnames):
    SEG = NT * T
    xpr = np.asarray(inp["x_prompt"], dtype=np.float32)
    ident = np.eye(P, dtype=np.float32)
    maskc = np.triu(np.ones((64, 64), dtype=np.float32))
    gains = np.stack([_pc(inp["norm_mix"][0]), _pc(inp["norm_xattn"][0]), _pc(inp["norm_ffn"][0]),
                      _pc(inp["norm_mem"][0]), _pc(inp["norm_final"])], axis=1).reshape(P, 5 * KD)
    lbl = np.ascontiguousarray(np.asarray(inp["lb_logits"], np.float32).reshape(2, NH, P).transpose(2, 0, 1)).reshape(P, 2 * NH)
    hgn = _pc(inp["hg_norm"][0])
    cw = np.ascontiguousarray(np.asarray(inp["conv_w"][0], np.float32).reshape(3, NH, P).transpose(2, 0, 1)).reshape(P, 3 * NH)
    shared = {"ident": ident, "maskc": maskc, "gains": np.ascontiguousarray(gains), "lbl": lbl, "hgn": hgn, "cw": cw}
    for n in wnames:
        shared[n] = np.ascontiguousarray(np.asarray(inp[n][0], dtype=np.float32))
    maps = []
    for c in range(8):
        b, j = c // 4, c % 4
        m = dict(shared)
        m["xp"] = np.ascontiguousarray(xpr[b, j * SEG:(j + 1) * SEG])
        xpred = np.zeros((3 * SEG, D), np.float32)
        if j > 0:
            xpred[(3 - j) * SEG:] = xpr[b, 0:j * SEG]
        m["xpred"] = xpred
        xprev = np.zeros((2, D), np.float32)
        if j > 0:
            xprev[:] = xpr[b, j * SEG - 2:j * SEG]
        m["xprev"] = xprev
        m["xs"] = np.ascontiguousarray(np.asarray(inp["x_sample"][c], np.float32))
        m["cmk"] = np.ascontiguousarray(np.asarray(inp["cache_mem_k"][0, c], np.float32).reshape(NMEM, D))
        m["cmv"] = np.ascontiguousarray(np.asarray(inp["cache_mem_v"][0, c], np.float32).reshape(NMEM, D))
        m["sh"] = np.ascontiguousarray(np.asarray(inp["state_hgrn"][0, c], np.float32).transpose(1, 0, 2)).reshape(P, NH * 128)
        m["sc"] = np.ascontiguousarray(np.asarray(inp["state_conv"][0, c], np.float32).reshape(2, NH, P).transpose(2, 1, 0)).reshape(P, 32)
        m["memp"] = np.ascontiguousarray(np.asarray(inp["mem_prompt"][b], np.float32))
        maps.append(m)
    return maps


def assemble(results, NT):
    SEG = NT * T
    L = 4 * SEG
    y_prompt = np.zeros((2, L, D), np.float32)
    y_sample = np.zeros((8, TS, D), np.float32)
    mk = np.zeros((1, 2, NMEM, XH, D // XH), np.float32)
    mv = np.zeros((1, 2, NMEM, XH, D // XH), np.float32)
    hp = np.zeros((1, 2, NH, 128, 128), np.float32)
    cp = np.zeros((1, 2, 2, HW), np.float32)
    hs = np.zeros((1, 8, NH, 128, 128), np.float32)
    cs = np.zeros((1, 8, 2, HW), np.float32)
    st = lambda a: np.asarray(a).reshape(P, NH, 128).transpose(1, 0, 2)
    cv = lambda a: np.asarray(a).reshape(P, NH, 2).transpose(2, 1, 0).reshape(2, HW)
    for c in range(8):
        b, j = c // 4, c % 4
        r = results[c]
        y_prompt[b, j * SEG:(j + 1) * SEG] = r["yp"]
        y_sample[c] = r["ys"]
        hs[0, c] = st(r["hs"])
        cs[0, c] = cv(r["cs"])
        if j == 0:
            mk[0, b] = np.asarray(r["mk"]).reshape(NMEM, XH, D // XH)
            mv[0, b] = np.asarray(r["mv"]).reshape(NMEM, XH, D // XH)
        if j == 3:
            hp[0, b] = st(r["hp"])
            cp[0, b] = cv(r["cp"])
    return (y_prompt, y_sample, mk, mv, hp, cp, hs, cs)


def run(inp, NT, debug=False, stages=None):
    import time, sys
    t0 = time.time()
    bld = Builder(NT, 3 * NT, debug=debug, stages=stages)
    nc = bld.build()
    t1 = time.time()
    maps = make_in_maps(inp, NT, list(bld.w.keys()))
    t2 = time.time()
    res = run_bass_kernel_spmd(nc, maps, core_ids=list(range(8)))
    print(f"[kernel] build {t1 - t0:.1f}s maps {t2 - t1:.1f}s run {time.time() - t2:.1f}s nops={ {e: len(v) for e, v in bld.s.ops.items()} }", file=sys.stderr)
    outs = assemble(res.results, NT)
    if debug:
        return outs, [r["dbg"] for r in res.results]
    return outs


def kernel(**inputs):
    return run(inputs, 8)
```

```python
import numpy as np
import concourse.bass as bass
import concourse.mybir as mybir
from concourse.bass_utils import run_bass_kernel_spmd

F32 = mybir.dt.float32
BF16 = mybir.dt.bfloat16
AF = mybir.ActivationFunctionType
ALU = mybir.AluOpType
AX = mybir.AxisListType

P = 128
D = 4096
KD = 32
HW = 2048
NH = 16
FF = 11008
KF = 86
NMEM = 256
XH = 4
PROJ = 22528
EPS = 1e-6
T = 256
TS = 32
C_Q, C_F, C_IV, C_OG, C_CH, C_CB, C_CC, C_GA, C_GB = 0, 2048, 4096, 6144, 8192, 10240, 12288, 14336, 18432
NRING = 3
SLABW = 256


class Op:
    __slots__ = ("eng", "fn", "deps", "signal", "ev", "dma_sem", "idx", "inc")

    def __init__(self, eng, fn, dma_sem, inc):
        self.eng = eng
        self.fn = fn
        self.deps = []
        self.signal = False
        self.ev = None
        self.dma_sem = dma_sem
        self.inc = inc


class Sched:
    ENGS = ("pe", "act", "dve", "pool", "sp")

    def __init__(self):
        self.dry = False
        self.reset()

    def reset(self):
        self.ops = {e: [] for e in self.ENGS}
        self.last_writer = {}
        self.readers = {}
        self.n = 0
        self.plan = []
        self.plan_pos = 0
        self.issued = 0

    def op(self, eng, fn, reads=(), writes=(), dma_sem=None, inc=16, after_all=False):
        if self.dry:
            return None
        o = Op(eng, fn, dma_sem, inc)
        o.idx = self.n
        self.n += 1
        psr = [k for k in reads if isinstance(k, tuple) and k[0] == "ps"]
        if psr:
            reads = [k for k in reads if not (isinstance(k, tuple) and k[0] == "ps")]
            writes = list(writes) + psr
        deps = {}
        if after_all:
            for e_ in ("pe", "act", "dve"):
                if self.ops[e_]:
                    w = self.ops[e_][-1]
                    deps[id(w)] = w
        lw = self.last_writer
        rd = self.readers
        for k in reads:
            w = lw.get(k)
            if w is not None:
                deps[id(w)] = w
        for k in writes:
            w = lw.get(k)
            if w is not None:
                deps[id(w)] = w
            for r in rd.get(k, ()):
                deps[id(r)] = r
        best = {}
        out = []
        for d in deps.values():
            if d is o:
                continue
            if d.dma_sem is not None:
                out.append(d)
            else:
                b = best.get(d.eng)
                if b is None or d.idx > b.idx:
                    best[d.eng] = d
        for e, d in best.items():
            if e == "pe" and eng == "pe" and dma_sem is None:
                continue
            out.append(d)
        for d in out:
            d.signal = True
        o.deps = out
        for k in writes:
            lw[k] = o
            rd[k] = []
        for k in reads:
            rd.setdefault(k, []).append(o)
        self.ops[eng].append(o)
        return o

    def fence(self, old_keys, new_keys):
        if self.dry:
            return
        prev = []
        for k in old_keys:
            w = self.last_writer.get(k)
            if w is not None:
                prev.append(w)
            prev.extend(self.readers.get(k, ()))
        for k in new_keys:
            self.last_writer.pop(k, None)
            self.readers[k] = list(prev)

    def emit(self, nc, engines, sems):
        cnt = {}
        for e in self.ENGS:
            c = 0
            for o in self.ops[e]:
                if o.dma_sem is not None:
                    v = cnt.get(o.dma_sem, 0) + o.inc
                    cnt[o.dma_sem] = v
                    o.ev = (o.dma_sem, v)
                elif o.signal:
                    c += 1
                    o.ev = (sems[e], c)

        def run(ename, eng):
            waited = {}
            for o in self.ops[ename]:
                for d in o.deps:
                    s, v = d.ev
                    key = id(s)
                    if waited.get(key, 0) < v:
                        eng.wait_ge(s, v)
                        waited[key] = v
                ins = o.fn(eng)
                if o.dma_sem is not None:
                    ins.then_inc(o.dma_sem, o.inc)
                elif o.signal:
                    ins.then_inc(sems[ename], 1)
        return run


class Builder:
    def __init__(self, NT, NP1, debug=False, stages=None):
        self.stages = stages if stages is not None else {"p1", "memkv", "tile", "sample", "mixer", "xattn", "ffn"}
        self.NT = NT
        self.NP1 = NP1
        self.SEG = NT * T
        self.debug = debug
        nc = bass.Bass("TRN2", target_bir_lowering=False)
        self.nc = nc
        self.s = Sched()
        self._declare_dram()
        self._alloc()

    def _declare_dram(self):
        nc = self.nc
        di = lambda n, sh: nc.dram_tensor(n, sh, F32, kind="ExternalInput").ap()
        do = lambda n, sh: nc.dram_tensor(n, sh, F32, kind="ExternalOutput").ap()
        SEG = self.SEG
        self.d_xp = di("xp", [SEG, D])
        self.d_xpred = di("xpred", [max(self.NP1, 1) * T, D])
        self.d_xprev = di("xprev", [2, D])
        self.d_xs = di("xs", [TS, D])
        self.d_cmk = di("cmk", [NMEM, D])
        self.d_cmv = di("cmv", [NMEM, D])
        self.d_sh = di("sh", [P, NH * 128])
        self.d_sc = di("sc", [P, 32])
        self.d_memp = di("memp", [NMEM, D])
        self.d_ident = di("ident", [P, P])
        self.d_mask = di("maskc", [64, 64])
        self.d_gains = di("gains", [P, 5 * KD])
        self.d_lbl = di("lbl", [P, 2 * NH])
        self.d_hgn = di("hgn", [P, NH])
        self.d_cw = di("cw", [P, 3 * NH])
        self.w = {}
        self.wshapes = dict((("w_in", [D, PROJ]), ("w_a", [HW, D]), ("w_b", [HW, D]), ("w_o", [D, D]),
                             ("w_xq", [D, D]), ("w_xk", [D, D]), ("w_xv", [D, D]), ("w_xo", [D, D]),
                             ("w_gate", [D, FF]), ("w_up", [D, FF]), ("w_down", [FF, D])))
        self.o_yp = do("yp", [SEG, D])
        self.o_ys = do("ys", [TS, D])
        self.o_mk = do("mk", [NMEM, D])
        self.o_mv = do("mv", [NMEM, D])
        self.o_hp = do("hp", [P, NH * 128])
        self.o_cp = do("cp", [P, 32])
        self.o_hs = do("hs", [P, NH * 128])
        self.o_cs = do("cs", [P, 32])
        if self.debug:
            self.o_dbg = do("dbg", [P, 4 * KD * T])

    def _alloc(self):
        nc = self.nc
        ARENA = 212800 // 4
        arena = nc.alloc_sbuf_tensor("arena", [P, ARENA], F32)
        self.A = arena.ap()
        self.Ab = self.A.bitcast(BF16)
        self.off = 0

        def f32(nbytes_elems, shape=None):
            n = nbytes_elems
            a = self.off // 4
            self.off += n * 4
            ap = self.A[:, a:a + n]
            return ap

        def bf(n):
            n2 = (n + 1) // 2 * 2
            a = self.off // 2
            self.off += n2 * 2
            return self.Ab[:, a:a + n]

        self.f32 = f32
        self.bf = bf
        r3 = lambda ap, k: ap.rearrange("p (k t) -> p k t", k=k)
        self.xT = r3(f32(KD * T), KD)
        self.hT = r3(bf(KD * T), KD)
        self.S = r3(f32(NH * 128), NH)
        self.Sb = r3(bf(NH * 128), NH)
        self.KT = r3(bf(KD * NMEM), KD)
        self.V = r3(bf(2 * D), 2)
        self.ring = [r3(bf(KD * SLABW), KD) for _ in range(NRING)]
        self.ident_f = f32(128)
        self.ident_b = bf(128)
        self.ones_b = bf(128)
        self.mask = f32(64)
        self.gains = r3(f32(5 * KD), 5)
        self.lbl = r3(f32(2 * NH), 2)
        self.lb = f32(NH)
        self.oml = f32(NH)
        self.hgn = f32(NH)
        self.cw = r3(f32(3 * NH), 3)
        self.rmask = f32(2 * T)
        self.rmask32 = f32(2 * TS)
        self.carry = r3(f32(NH * 2), NH)
        self.carry_s = r3(f32(NH * 2), NH)
        self.chp = r3(f32(4), 2)
        self.xprevT = r3(f32(KD * 2), KD)
        self.hprevT = r3(bf(KD * 2), KD)
        self.tiny = f32(16)
        self.sq = [r3(bf(4 * T), 4) for _ in range(2)]
        self.r0 = f32(2 * T)
        self.R = f32(2 * T)
        self.scratch0 = self.off
        self.scratch_bytes = ARENA * 4 - self.off
        assert self.scratch_bytes >= 53700, self.scratch_bytes
        self.psb = []
        for i in range(8):
            if i == 4:
                self.pbf = nc.alloc_psum_tensor("pbf", [P, 1024], BF16).ap()
                self.psb.append(None)
            else:
                self.psb.append(nc.alloc_psum_tensor(f"psb{i}", [P, 512], F32).ap())
        sem = lambda n: nc.alloc_semaphore(n)
        self.sems = {e: sem("e_" + e) for e in Sched.ENGS}
        self.sem_ring = [sem(f"ring{i}") for i in range(NRING)]
        self.sem_xst = [sem(f"xst{i}") for i in range(2)]
        self.sem_wb = [sem(f"wb{i}") for i in range(NRING)]
        self.sem_kst = [sem(f"kst{i}") for i in range(2)]
        self.sem_c = [sem(f"c{i}") for i in range(6)]
        self.sem_misc = [sem(f"m{i}") for i in range(8)]

    def scratch(self):
        self.off = self.scratch0

    def next_slab(self, wname, k0, nk, c0, ncols):
        s = self.s
        spec = (wname, k0, nk, c0, ncols)
        if s.dry:
            if wname not in self.w:
                self.w[wname] = self.nc.dram_tensor(wname, self.wshapes[wname], F32, kind="ExternalInput").ap()
            s.plan.append(spec)
            self.wc_count[spec] = self.wc_count.get(spec, 0) + 1
            return 0
        i = s.plan_pos
        assert s.plan[i] == spec, (s.plan[i], spec)
        s.plan_pos += 1
        while s.issued < len(s.plan) and s.issued <= i + NRING - 1:
            self._issue_slab(s.issued)
            s.issued += 1
        return i % NRING

    def _issue_slab(self, i):
        spec = self.s.plan[i]
        wname, k0, nk, c0, ncols = spec
        slot = i % NRING
        dst = self.ring[slot][:, 0:nk, 0:ncols]
        off = self.wc_off.get(spec)
        if off is not None and spec in self.wc_written:
            src = self.wcaches[off[0]][:, off[1]:off[1] + nk * ncols].rearrange("p (k n) -> p k n", k=nk)
            self.s.op("pool", lambda e, d=dst, s_=src: e.dma_start(out=d, in_=s_),
                      reads=[("wc", off)], writes=[("ring", slot)], dma_sem=self.sem_ring[slot])
            return
        src = self.w[wname][k0 * P:(k0 + nk) * P, c0:c0 + ncols].rearrange("(k p) n -> p k n", p=P)
        self.s.op("pool", lambda e, d=dst, s_=src: e.dma_start(out=d, in_=s_),
                  writes=[("ring", slot)], dma_sem=self.sem_ring[slot])
        if off is not None:
            wdst = self.wcaches[off[0]][:, off[1]:off[1] + nk * ncols].rearrange("p (k n) -> p k n", k=nk)
            self.s.op("sp", lambda e, d=wdst, s_=dst: e.dma_start(out=d, in_=s_),
                      reads=[("ring", slot)], writes=[("wc", off)], dma_sem=self.sem_wb[slot])
            self.wc_written.add(spec)

    def evac_engine(self):
        self._alt = 1 - getattr(self, "_alt", 0)
        return "act" if self._alt else "dve"

    def copy(self, eng, out, in_, reads, writes):
        if eng == "act":
            self.s.op("act", lambda e: e.activation(out=out, in_=in_, func=AF.Copy), reads=reads, writes=writes)
        else:
            self.s.op("dve", lambda e: e.tensor_copy(out=out, in_=in_), reads=reads, writes=writes)

    def ps_acc(self, i):
        b = i % 4
        return self.psb[b].rearrange("p (j t) -> p j t", j=2), ("ps", b)

    def load_xT(self, src_rows, n, xT, xkey):
        s = self.s
        self.scratch()
        st = [self.f32(D) for _ in range(2)]
        nblk = (n + P - 1) // P
        for blk in range(nblk):
            rows = min(P, n - blk * P)
            stg = st[blk % 2]
            skey = ("xst", blk % 2)
            src = src_rows[blk * P:blk * P + rows, :]
            s.op("sp", lambda e, d=stg[0:rows, :], s_=src: e.dma_start(out=d, in_=s_),
                 writes=[skey], dma_sem=self.sem_xst[blk % 2], after_all=True)
            for cg in range(8):
                bank = 5 + (cg % 2)
                tp = self.psb[bank].rearrange("p (i t) -> p i t", i=4)
                for i in range(4):
                    c = 4 * cg + i
                    s.op("pe", lambda e, o=tp[:, i, 0:rows], a=stg[0:rows, c * P:(c + 1) * P], r=rows:
                         e.transpose(out=o, in_=a, identity=self.ident_f[0:r, 0:r]),
                         reads=[skey, "consts"], writes=[("ps", bank)])
                self.copy(self.evac_engine(), xT[:, 4 * cg:4 * cg + 4, blk * P:blk * P + rows], tp[:, :, 0:rows],
                          reads=[("ps", bank)], writes=[xkey])

    def norm(self, xT, xkey, gidx, n, out, okey, out_scale=None):
        s = self.s
        ssq = self.psb[6][:, 0:n]
        for grp in range(8):
            sq = self.sq[grp % 2]
            s.op("act", lambda e, o=sq[:, :, 0:n], a=xT[:, 4 * grp:4 * grp + 4, 0:n]:
                 e.activation(out=o, in_=a, func=AF.Square), reads=[xkey], writes=[("sq", grp % 2)])
            for i in range(4):
                s.op("pe", lambda e, a=sq[:, i, 0:n], st=(grp == 0 and i == 0), sp=(grp == 7 and i == 3):
                     e.matmul(ssq, lhsT=self.ones_b, rhs=a, start=st, stop=sp),
                     reads=[("sq", grp % 2), "consts"], writes=[("ps", 6)])
        r0 = self.r0[:, 0:n]
        R = self.R[:, 0:n]
        s.op("act", lambda e: e.activation(out=r0, in_=ssq, func=AF.Sqrt, bias=EPS, scale=1.0 / D),
             reads=[("ps", 6)], writes=["r0"])
        s.op("dve", lambda e: e.reciprocal(out=R, in_=r0), reads=["r0"], writes=["R"])
        for k in range(KD):
            s.op("dve", lambda e, o=out[:, k, 0:n], a=xT[:, k, 0:n], g=self.gains[:, gidx, k:k + 1]:
                 e.scalar_tensor_tensor(out=o, in0=a, scalar=g, in1=R, op0=ALU.mult, op1=ALU.mult),
                 reads=[xkey, "R", "consts"], writes=[okey])

    def proj(self, wname, k0, nk, c0, ncols, rhs_list, evac):
        s = self.s
        pieces = []
        kk = 0
        while kk < nk:
            m = min(KD, nk - kk)
            pieces.append((kk, m))
            kk += m
        for cs in range(0, ncols, SLABW):
            w_ = min(SLABW, ncols - cs)
            nj = w_ // P
            self._acc_i = getattr(self, "_acc_i", 0) + 1
            acc, pskey = self.ps_acc(self._acc_i)
            accs = [(acc[:, j, :], pskey) for j in range(2)]
            if len(pieces) > 1:
                self._acc_i += 1
                acc2, pskey2 = self.ps_acc(self._acc_i)
                accs = [(acc[:, 0, :], pskey), (acc2[:, 0, :], pskey2)]
            for pi, (pk, pm) in enumerate(pieces):
                slot = self.next_slab(wname, k0 + pk, pm, c0 + cs, w_)
                slab = self.ring[slot]
                for j in range(nj):
                    for k in range(pm):
                        first = (pi == 0 and k == 0)
                        last = (pi == len(pieces) - 1 and k == pm - 1)
                        for g, (rhs_fn, n, rkeys) in enumerate(rhs_list):
                            if g == 0:
                                o = accs[j][0][:, 0:n]
                                wk = [accs[j][1]]
                            else:
                                o = self.psb[7][:, (j * 4 + g) * 32:(j * 4 + g) * 32 + n]
                                wk = [("ps", 7)]
                            s.op("pe", lambda e, o=o, l=slab[:, k, j * P:(j + 1) * P], r=rhs_fn(pk + k), st=first, sp=last:
                                 e.matmul(o, lhsT=l, rhs=r, start=st, stop=sp),
                                 reads=[("ring", slot)] + list(rkeys), writes=wk)
            for j in range(nj):
                oc = (c0 + cs) // P + j
                for g, (rhs_fn, n, rkeys) in enumerate(rhs_list):
                    if g == 0:
                        evac(g, oc, j, accs[j][0][:, 0:n], accs[j][1])
                    else:
                        evac(g, oc, j, self.psb[7][:, (j * 4 + g) * 32:(j * 4 + g) * 32 + n], ("ps", 7))

    def hgrn_alloc(self, n, C):
        nch = n // C
        r2 = lambda ap: ap.rearrange("p (j t) -> p j t", j=2)
        h = {}
        h["Q"] = r2(self.f32(2 * n)); h["G"] = r2(self.f32(2 * n)); h["KK"] = r2(self.f32(2 * n)); h["D1"] = r2(self.f32(2 * n))
        h["B"] = r2(self.f32(2 * n))
        for nm in ("VT", "OGS", "QT", "KT", "KH", "SQ"):
            h[nm] = r2(self.bf(2 * n))
        h["ktv"] = self.bf(nch * 4 * 128).rearrange("p (c i v) -> p c i v", c=nch, i=4)
        h["scm"] = r2(self.bf(2 * C))
        h["DD"] = self.f32(2 * nch).rearrange("p (j c) -> p j c", j=2)
        h["bsum"] = self.f32(2)
        return h

    def hgrn_group(self, hg, n, C, h, hT, state_only, oN=None):
        s = self.s
        nch = n // C
        h0 = 2 * hg
        rhs = [(lambda k: hT[:, k, 0:n], n, ["hT"])]
        Q, G, KK, D1, B = h["Q"], h["G"], h["KK"], h["D1"], h["B"]
        VT, OGS, QT, KT, KH, SQ = h["VT"], h["OGS"], h["QT"], h["KT"], h["KH"], h["SQ"]
        ktv, scm, DD = h["ktv"], h["scm"], h["DD"]

        def ev_act(dst, key, func):
            def f(g, oc, j, ps, pskey):
                s.op("act", lambda e: e.activation(out=dst[:, j, :], in_=ps, func=func), reads=[pskey], writes=[key])
            return f

        def ev_dve(dst, key):
            def f(g, oc, j, ps, pskey):
                s.op("dve", lambda e: e.tensor_copy(out=dst[:, j, :], in_=ps), reads=[pskey], writes=[key])
            return f

        if not state_only:
            self.proj("w_in", 0, KD, C_Q + hg * 256, 256, rhs, ev_act(Q, "hQ", AF.Copy))
        self.proj("w_in", 0, KD, C_F + hg * 256, 256, rhs, ev_act(G, "hG", AF.Sigmoid))
        self.proj("w_in", 0, KD, C_IV + hg * 256, 256, rhs, ev_dve(VT, "hVT"))
        if not state_only:
            self.proj("w_in", 0, KD, C_OG + hg * 256, 256, rhs, ev_act(OGS, "hOGS", AF.Silu))
        for j in range(2):
            hh = h0 + j
            s.op("dve", lambda e, a=G[:, j, :], m=self.oml[:, hh:hh + 1], b=self.lb[:, hh:hh + 1]:
                 e.tensor_scalar(out=a, in0=a, scalar1=m, scalar2=b, op0=ALU.mult, op1=ALU.add),
                 reads=["hG", "consts"], writes=["hG"])
        s.op("dve", lambda e: e.tensor_scalar(out=KK, in0=G, scalar1=-1.0, scalar2=1.0, op0=ALU.mult, op1=ALU.add),
             reads=["hG"], writes=["hKK"])
        s.op("act", lambda e: e.activation(out=G, in_=G, func=AF.Ln), reads=["hG"], writes=["hG"])
        Gf = G.rearrange("p j t -> p (j t)")
        Bf = B.rearrange("p j t -> p (j t)")
        rm = self.rmask_C[C][:, 0:2 * n]
        s.op("dve", lambda e: e.tensor_tensor_scan(out=Bf, data0=rm, data1=Gf, initial=0.0, op0=ALU.mult, op1=ALU.add),
             reads=["hG", "consts"], writes=["hB"])
        B4 = B.rearrange("p j (c t) -> p j c t", c=nch)
        bl = B4[:, :, :, C - 1:C]
        if not state_only:
            s.op("act", lambda e: e.activation(out=D1, in_=B, func=AF.Exp), reads=["hB"], writes=["hD1"])
            s.op("dve", lambda e: e.tensor_tensor(out=QT, in0=Q, in1=D1, op=ALU.mult), reads=["hQ", "hD1"], writes=["hQT"])
            s.op("act", lambda e: e.activation(out=D1, in_=B, func=AF.Exp, scale=-1.0), reads=["hB", "hQT"], writes=["hD1"])
            s.op("dve", lambda e: e.tensor_tensor(out=KT, in0=KK, in1=D1, op=ALU.mult), reads=["hKK", "hD1"], writes=["hKT"])
        D14 = D1.rearrange("p j (c t) -> p j c t", c=nch)
        s.op("dve", lambda e: e.tensor_tensor(out=D14, in0=bl.broadcast_to([P, 2, nch, C]), in1=B4, op=ALU.subtract),
             reads=["hB", "hKT"], writes=["hD1"])
        s.op("act", lambda e: e.activation(out=D1, in_=D1, func=AF.Exp), reads=["hD1"], writes=["hD1"])
        s.op("dve", lambda e: e.tensor_tensor(out=KH, in0=KK, in1=D1, op=ALU.mult), reads=["hKK", "hD1"], writes=["hKH"])
        s.op("act", lambda e: e.activation(out=DD, in_=B4[:, :, :, C - 1], func=AF.Exp),
             reads=["hB"], writes=["hDD"])
        tpk = self.pbf[:, 0:512].rearrange("p (i v) -> p i v", i=4)
        for c in range(nch):
            for j in range(2):
                s.op("pe", lambda e, o=tpk[0:C, j, :], a=KH[:, j, c * C:(c + 1) * C]:
                     e.transpose(out=o, in_=a, identity=self.ident_b), reads=["hKH", "consts"], writes=[("ps", 4)])
                s.op("pe", lambda e, o=tpk[0:C, 2 + j, :], a=VT[:, j, c * C:(c + 1) * C]:
                     e.transpose(out=o, in_=a, identity=self.ident_b), reads=["hVT", "consts"], writes=[("ps", 4)])
            self.copy(self.evac_engine(), ktv[0:C, c, :, :], tpk[0:C, :, :], reads=[("ps", 4)], writes=["hktv"])
        OT = Q
        sc = self.psb[5][:, 0:2 * C].rearrange("p (j t) -> p j t", j=2)
        po = self.psb[6][:, 0:2 * C].rearrange("p (j t) -> p j t", j=2)
        pst = self.psb[7][:, 0:256].rearrange("p (j v) -> p j v", j=2)
        for c in range(nch):
            cs = slice(c * C, (c + 1) * C)
            if not state_only:
                for j in range(2):
                    s.op("pe", lambda e, o=sc[0:C, j, :], l=KT[:, j, cs], r=QT[:, j, cs]:
                         e.matmul(o, lhsT=l, rhs=r, start=True, stop=True), reads=["hKT", "hQT"], writes=[("ps", 5)])
                s.op("dve", lambda e: e.tensor_tensor(out=scm[0:C, :, :], in0=sc[0:C, :, :],
                                                      in1=self.mask_C[C].unsqueeze(1).broadcast_to([C, 2, C]), op=ALU.mult),
                     reads=[("ps", 5), "consts"], writes=["hscm"])
                for j in range(2):
                    s.op("pe", lambda e, o=po[:, j, :], l=self.Sb[:, h0 + j, :], r=QT[:, j, cs]:
                         e.matmul(o, lhsT=l, rhs=r, start=True, stop=False), reads=["Sb", "hQT"], writes=[("ps", 6)])
                    s.op("pe", lambda e, o=po[:, j, :], l=ktv[0:C, c, 2 + j, :], r=scm[0:C, j, :]:
                         e.matmul(o, lhsT=l, rhs=r, start=False, stop=True), reads=["hktv", "hscm"], writes=[("ps", 6)])
                s.op("act", lambda e, o=OT[:, :, cs]: e.activation(out=o, in_=po, func=AF.Copy),
                     reads=[("ps", 6), "hQT"], writes=["hQ"])
            for j in range(2):
                s.op("pe", lambda e, o=pst[:, j, :], l=ktv[0:C, c, j, :], r=ktv[0:C, c, 2 + j, :]:
                     e.matmul(o, lhsT=l, rhs=r, start=True, stop=True), reads=["hktv"], writes=[("ps", 7)])
            for j in range(2):
                s.op("dve", lambda e, a=self.S[:, h0 + j, :], d=DD[:, j, c:c + 1], p_=pst[:, j, :]:
                     e.scalar_tensor_tensor(out=a, in0=a, scalar=d, in1=p_, op0=ALU.mult, op1=ALU.add),
                     reads=[("ps", 7), "hDD", "S"], writes=["S"])
            if not state_only:
                s.op("act", lambda e: e.activation(out=self.Sb[:, h0:h0 + 2, :], in_=self.S[:, h0:h0 + 2, :], func=AF.Copy),
                     reads=["S"], writes=["Sb"])
        if state_only:
            return
        s.op("act", lambda e: e.activation(out=SQ, in_=OT, func=AF.Square), reads=["hQ"], writes=["hSQ"])
        ssq = self.psb[5][:, 0:2 * n].rearrange("p (j t) -> p j t", j=2)
        for j in range(2):
            s.op("pe", lambda e, o=ssq[:, j, :], a=SQ[:, j, :]: e.matmul(o, lhsT=self.ones_b, rhs=a, start=True, stop=True),
                 reads=["hSQ", "consts"], writes=[("ps", 5)])
        s.op("act", lambda e: e.activation(out=D1, in_=ssq, func=AF.Sqrt, bias=EPS, scale=1.0 / 128),
             reads=[("ps", 5)], writes=["hD1"])
        s.op("dve", lambda e: e.reciprocal(out=D1, in_=D1), reads=["hD1"], writes=["hD1"])
        for j in range(2):
            hh = h0 + j
            s.op("dve", lambda e, a=OT[:, j, :], g=self.hgn[:, hh:hh + 1], r=D1[:, j, :]:
                 e.scalar_tensor_tensor(out=a, in0=a, scalar=g, in1=r, op0=ALU.mult, op1=ALU.mult),
                 reads=["hQ", "hD1", "consts"], writes=["hQ"])
        s.op("dve", lambda e: e.tensor_tensor(out=oN[:, h0:h0 + 2, 0:n], in0=OT, in1=OGS, op=ALU.mult),
             reads=["hQ", "hOGS"], writes=["oN"])

    def mixer(self, n, C, carry, ckey, with_prev):
        s = self.s
        xT, hT = self.xT, self.hT
        self.norm(xT, "xT", 0, n, hT, "hT")
        if with_prev:
            self.norm(self.xprevT, "xprevT", 0, 2, self.hprevT, "hprevT")
        self.scratch()
        r3 = lambda ap, k: ap.rearrange("p (k t) -> p k t", k=k)
        oN = r3(self.bf(NH * n), NH)
        cbz = r3(self.bf(NH * n), NH)
        merged = r3(self.bf(KD * n), KD)
        grp0 = self.off
        h = self.hgrn_alloc(n, C)
        s.fence(["mSGA", "mSGB", "mTA", "mTB", "cCH", "cU"], ["hQ", "hG", "hKK", "hD1", "hB", "hVT", "hOGS", "hQT", "hKT", "hKH", "hSQ", "hktv", "hscm", "hDD"])
        for hg in range(NH // 2):
            self.hgrn_group(hg, n, C, h, hT, False, oN)
        self.off = grp0
        s.fence(["hQ", "hG", "hKK", "hD1", "hB", "hVT", "hOGS", "hQT", "hKT", "hKH", "hSQ", "hktv", "hscm", "hDD"], ["cCH", "cU"])
        r2 = lambda ap: ap.rearrange("p (j t) -> p j t", j=2)
        CH = r2(self.f32(2 * n))
        U = r2(self.f32(2 * (n + 2)))
        rhs = [(lambda k: hT[:, k, 0:n], n, ["hT"])]
        if with_prev:
            rhs.append((lambda k: self.hprevT[:, k, 0:2], 2, ["hprevT"]))
        for cp in range(NH // 2):
            def ev_ch(g, oc, j, ps, pskey):
                if g == 0:
                    s.op("act", lambda e: e.activation(out=CH[:, j, :], in_=ps, func=AF.Copy), reads=[pskey], writes=["cCH"])
                else:
                    s.op("act", lambda e: e.activation(out=self.chp[:, j, :], in_=ps, func=AF.Copy), reads=[pskey], writes=["chp"])

            def ev_cc(g, oc, j, ps, pskey):
                cidx = 2 * cp + j
                if g == 0:
                    s.op("dve", lambda e: e.tensor_tensor(out=U[:, j, 2:n + 2], in0=ps, in1=CH[:, j, :], op=ALU.mult),
                         reads=[pskey, "cCH"], writes=["cU"])
                else:
                    s.op("dve", lambda e: e.tensor_tensor(out=carry[:, cidx, :], in0=ps, in1=self.chp[:, j, :], op=ALU.mult),
                         reads=[pskey, "chp"], writes=[ckey])

            def ev_cb(g, oc, j, ps, pskey):
                cidx = 2 * cp + j
                s.op("dve", lambda e: e.tensor_tensor(out=cbz[:, cidx, 0:n], in0=ps, in1=CH[:, j, :], op=ALU.mult),
                     reads=[pskey, "cCH"], writes=["cbz"])

            self.proj("w_in", 0, KD, C_CH + cp * 256, 256, rhs, ev_ch)
            self.proj("w_in", 0, KD, C_CC + cp * 256, 256, rhs, ev_cc)
            for j in range(2):
                cidx = 2 * cp + j
                s.op("dve", lambda e, j=j, cidx=cidx: e.tensor_copy(out=U[:, j, 0:2], in_=carry[:, cidx, :]),
                     reads=[ckey], writes=["cU"])
                for tap in range(3):
                    wv = self.cw[:, tap, cidx:cidx + 1]
                    if tap == 0:
                        s.op("dve", lambda e, j=j, wv=wv: e.tensor_scalar(out=CH[:, j, :], in0=U[:, j, 0:n], scalar1=wv,
                                                                          scalar2=None, op0=ALU.mult),
                             reads=["cU", "consts"], writes=["cCH"])
                    else:
                        s.op("dve", lambda e, j=j, wv=wv, tap=tap: e.scalar_tensor_tensor(
                            out=CH[:, j, :], in0=U[:, j, tap:tap + n], scalar=wv, in1=CH[:, j, :], op0=ALU.mult, op1=ALU.add),
                             reads=["cU", "cCH", "consts"], writes=["cCH"])
                s.op("act", lambda e, j=j, cidx=cidx: e.activation(out=carry[:, cidx, :], in_=U[:, j, n:n + 2], func=AF.Copy),
                     reads=["cU"], writes=[ckey])
            self.proj("w_in", 0, KD, C_CB + cp * 256, 256, rhs[:1], ev_cb)
        self.off = grp0
        s.fence(["cCH", "cU"], ["mSGA", "mSGB", "mTA", "mTB"])
        SGA = r2(self.f32(2 * n)); SGB = r2(self.f32(2 * n)); TA = r2(self.f32(2 * n)); TB = r2(self.f32(2 * n))
        rhs_h = [(lambda k: hT[:, k, 0:n], n, ["hT"])]
        rhs_o = [(lambda k: oN[:, k, 0:n], n, ["oN"])]
        rhs_c = [(lambda k: cbz[:, k, 0:n], n, ["cbz"])]
        for jp in range(KD // 2):
            def ev_sig(dst, key):
                def f(g, oc, j, ps, pskey):
                    s.op("act", lambda e: e.activation(out=dst[:, j, :], in_=ps, func=AF.Sigmoid), reads=[pskey], writes=[key])
                return f

            def ev_a(g, oc, j, ps, pskey):
                s.op("dve", lambda e: e.tensor_tensor(out=TA[:, j, :], in0=ps, in1=SGA[:, j, :], op=ALU.mult),
                     reads=[pskey, "mSGA"], writes=["mTA"])

            def ev_b(g, oc, j, ps, pskey):
                s.op("dve", lambda e: e.tensor_tensor(out=TB[:, j, :], in0=ps, in1=SGB[:, j, :], op=ALU.mult),
                     reads=[pskey, "mSGB"], writes=["mTB"])
                mdst = merged[:, oc, 0:n]
                s.op("dve", lambda e: e.tensor_tensor(out=mdst, in0=TA[:, j, :], in1=TB[:, j, :], op=ALU.add),
                     reads=["mTA", "mTB"], writes=["merged"])

            self.proj("w_in", 0, KD, C_GA + jp * 256, 256, rhs_h, ev_sig(SGA, "mSGA"))
            self.proj("w_in", 0, KD, C_GB + jp * 256, 256, rhs_h, ev_sig(SGB, "mSGB"))
            self.proj("w_a", 0, NH, jp * 256, 256, rhs_o, ev_a)
            self.proj("w_b", 0, NH, jp * 256, 256, rhs_c, ev_b)
        self.proj("w_o", 0, KD, 0, D, [(lambda k: merged[:, k, 0:n], n, ["merged"])], self.ev_resid(n))

    def ev_resid(self, n):
        s = self.s

        def f(g, oc, j, ps, pskey):
            s.op("dve", lambda e: e.tensor_tensor(out=self.xT[:, oc, 0:n], in0=self.xT[:, oc, 0:n], in1=ps, op=ALU.add),
                 reads=[pskey, "xT"], writes=["xT"])
        return f

    def xattn(self, n):
        s = self.s
        xT, hT = self.xT, self.hT
        self.norm(xT, "xT", 1, n, hT, "hT")
        self.scratch()
        r3 = lambda ap, k: ap.rearrange("p (k t) -> p k t", k=k)
        qT = r3(self.bf(KD * n), KD)
        aT = r3(self.bf(KD * n), KD)
        Pexp = [self.f32(NMEM) for _ in range(2)]
        Pn = [self.bf(NMEM) for _ in range(2)]
        PTs = [r3(self.bf(2 * n), 2) for _ in range(2)]
        st = [self.f32(4) for _ in range(2)]

        def ev_q(g, oc, j, ps, pskey):
            s.op("act", lambda e: e.activation(out=qT[:, oc, 0:n], in_=ps, func=AF.Copy), reads=[pskey], writes=["qT"])
        self.proj("w_xq", 0, KD, 0, D, [(lambda k: hT[:, k, 0:n], n, ["hT"])], ev_q)
        scale = 1.0 / 32.0
        nblk = (n + P - 1) // P
        it = 0
        pbf = self.pbf.rearrange("p (i t) -> p i t", i=8)
        for hd in range(XH):
            pt = PTs[hd % 2]
            ptkey = ("PTs", hd % 2)
            for tb in range(nblk):
                rows = min(P, n - tb * P)
                bank = 5 + (it % 2)
                sc = self.psb[bank][0:rows, 0:NMEM]
                pe_, pn_, st_ = Pexp[it % 2], Pn[it % 2], st[it % 2]
                kx = it % 2
                for kc in range(8):
                    s.op("pe", lambda e, l=qT[:, 8 * hd + kc, tb * P:tb * P + rows], r=self.KT[:, 8 * hd + kc, :], a=(kc == 0), b=(kc == 7), sc=sc:
                         e.matmul(sc, lhsT=l, rhs=r, start=a, stop=b), reads=["qT", "KT"], writes=[("ps", bank)])
                s.op("dve", lambda e, sc=sc, st_=st_, rows=rows: e.tensor_reduce(out=st_[0:rows, 0:1], in_=sc, axis=AX.X, op=ALU.max),
                     reads=[("ps", bank)], writes=[("ast", kx)])
                s.op("dve", lambda e, st_=st_, rows=rows: e.tensor_scalar(out=st_[0:rows, 1:2], in0=st_[0:rows, 0:1], scalar1=-scale,
                                                                          scalar2=None, op0=ALU.mult),
                     reads=[("ast", kx)], writes=[("ast", kx)])
                s.op("act", lambda e, sc=sc, st_=st_, pe_=pe_, rows=rows: e.activation(
                    out=pe_[0:rows, :], in_=sc, func=AF.Exp, bias=st_[0:rows, 1:2], scale=scale, accum_out=st_[0:rows, 2:3]),
                     reads=[("ps", bank), ("ast", kx)], writes=[("ast", kx), ("Pexp", kx)])
                s.op("dve", lambda e, st_=st_, rows=rows: e.reciprocal(out=st_[0:rows, 3:4], in_=st_[0:rows, 2:3]),
                     reads=[("ast", kx)], writes=[("ast", kx)])
                s.op("dve", lambda e, st_=st_, pe_=pe_, pn_=pn_, rows=rows: e.tensor_scalar(
                    out=pn_[0:rows, :], in0=pe_[0:rows, :], scalar1=st_[0:rows, 3:4], scalar2=None, op0=ALU.mult),
                     reads=[("ast", kx), ("Pexp", kx)], writes=[("Pn", kx)])
                for mh in range(2):
                    s.op("pe", lambda e, o=pbf[:, 4 + (it % 2) * 2 + mh, 0:rows], a=pn_[0:rows, mh * P:(mh + 1) * P], rows=rows:
                         e.transpose(out=o, in_=a, identity=self.ident_b[0:rows, 0:rows]),
                         reads=[("Pn", kx), "consts"], writes=[("ps", 4)])
                s.op("act", lambda e, o=pt[:, :, tb * P:tb * P + rows], a=pbf[:, 4 + (it % 2) * 2:4 + (it % 2) * 2 + 2, 0:rows]:
                     e.activation(out=o, in_=a, func=AF.Copy), reads=[("ps", 4)], writes=[ptkey])
                it += 1
            for dc in range(8):
                half = dc % 2
                pv = self.psb[7][:, half * 256:half * 256 + n]
                for mh in range(2):
                    s.op("pe", lambda e, pv=pv, l=self.V[:, mh, hd * 1024 + dc * P:hd * 1024 + (dc + 1) * P], r=pt[:, mh, 0:n], a=(mh == 0), b=(mh == 1):
                         e.matmul(pv, lhsT=l, rhs=r, start=a, stop=b), reads=["V", ptkey], writes=[("ps", 7)])
                s.op("act", lambda e, pv=pv, o=aT[:, 8 * hd + dc, 0:n]: e.activation(out=o, in_=pv, func=AF.Copy),
                     reads=[("ps", 7)], writes=["aT"])
        self.proj("w_xo", 0, KD, 0, D, [(lambda k: aT[:, k, 0:n], n, ["aT"])], self.ev_resid(n))

    def ffn(self, n):
        s = self.s
        xT, hT = self.xT, self.hT
        self.norm(xT, "xT", 2, n, hT, "hT")
        self.scratch()
        hid = self.bf(KF * n).rearrange("p (k t) -> p k t", k=KF)
        GS = self.f32(2 * n).rearrange("p (j t) -> p j t", j=2)
        rhs = [(lambda k: hT[:, k, 0:n], n, ["hT"])]
        for hp in range(KF // 2):
            def ev_g(g, oc, j, ps, pskey):
                s.op("act", lambda e: e.activation(out=GS[:, j, :], in_=ps, func=AF.Silu), reads=[pskey], writes=["fGS"])

            def ev_u(g, oc, j, ps, pskey):
                s.op("dve", lambda e: e.tensor_tensor(out=hid[:, oc, 0:n], in0=ps, in1=GS[:, j, :], op=ALU.mult),
                     reads=[pskey, "fGS"], writes=["hid"])
            self.proj("w_gate", 0, KD, hp * 256, 256, rhs, ev_g)
            self.proj("w_up", 0, KD, hp * 256, 256, rhs, ev_u)
        self.proj("w_down", 0, KF, 0, D, [(lambda k: hid[:, k, 0:n], n, ["hid"])], self.ev_resid(n))

    def final_out(self, n, dst_rows):
        s = self.s
        xT = self.xT
        self.norm(xT, "xT", 4, n, xT, "xT")
        self.scratch()
        st = [self.f32(D) for _ in range(2)]
        nblk = (n + P - 1) // P
        for blk in range(nblk):
            rows = min(P, n - blk * P)
            stg = st[blk % 2]
            skey = ("xst", blk % 2)
            for cg in range(8):
                bank = 5 + (cg % 2)
                tp = self.psb[bank].rearrange("p (i t) -> p i t", i=4)
                for i in range(4):
                    c = 4 * cg + i
                    s.op("pe", lambda e, o=tp[0:rows, i, :], a=xT[:, c, blk * P:blk * P + rows]:
                         e.transpose(out=o, in_=a, identity=self.ident_f), reads=["xT", "consts"], writes=[("ps", bank)])
                self.copy(self.evac_engine(), stg[0:rows, cg * 512:(cg + 1) * 512].rearrange("p (i t) -> p i t", i=4),
                          tp[0:rows, :, :], reads=[("ps", bank)], writes=[skey])
            s.op("sp", lambda e, d=dst_rows[blk * P:blk * P + rows, :], a=stg[0:rows, :]: e.dma_start(out=d, in_=a),
                 reads=[skey], writes=[("out", self._nout())], dma_sem=self.sem_xst[blk % 2])

    def _nout(self):
        self._outc = getattr(self, "_outc", 0) + 1
        self.outkeys.append(("out", self._outc))
        return self._outc

    def setup(self):
        s = self.s
        nc = self.nc
        ld = lambda dst, src, key="consts": s.op("sp", lambda e: e.dma_start(out=dst, in_=src), writes=[key], dma_sem=self._csem())
        ld(self.ident_f, self.d_ident)
        ld(self.mask[0:64, :], self.d_mask)
        ld(self.gains.rearrange("p g k -> p (g k)"), self.d_gains)
        ld(self.lbl.rearrange("p r h -> p (r h)"), self.d_lbl)
        ld(self.hgn, self.d_hgn)
        ld(self.cw.rearrange("p t c -> p (t c)"), self.d_cw)
        s.op("act", lambda e: e.activation(out=self.ident_b, in_=self.ident_f, func=AF.Copy), reads=["consts"], writes=["consts"])
        s.op("dve", lambda e: e.memset(self.ones_b, 1.0), writes=["consts"])
        s.op("dve", lambda e: e.memset(self.rmask, 1.0), writes=["consts"])
        s.op("dve", lambda e: e.memset(self.rmask.rearrange("p (n c) -> p n c", c=64)[:, :, 0:1], 0.0), writes=["consts"])
        s.op("dve", lambda e: e.memset(self.rmask32, 1.0), writes=["consts"])
        s.op("dve", lambda e: e.memset(self.rmask32.rearrange("p (n c) -> p n c", c=32)[:, :, 0:1], 0.0), writes=["consts"])
        s.op("dve", lambda e: e.tensor_tensor(out=self.lb, in0=self.lbl[:, 0, :], in1=self.lbl[:, 1, :], op=ALU.subtract),
             reads=["consts"], writes=["consts"])
        s.op("act", lambda e: e.activation(out=self.lb, in_=self.lb, func=AF.Sigmoid), reads=["consts"], writes=["consts"])
        s.op("dve", lambda e: e.tensor_scalar(out=self.oml, in0=self.lb, scalar1=-1.0, scalar2=1.0, op0=ALU.mult, op1=ALU.add),
             reads=["consts"], writes=["consts"])
        self.mask_C = {64: self.mask[0:64, 0:64], 32: self.mask[0:32, 0:32]}
        self.rmask_C = {64: self.rmask, 32: self.rmask32}

    def _csem(self):
        self._ci = getattr(self, "_ci", -1) + 1
        return self.sem_c[self._ci % len(self.sem_c)]

    def zero_state(self):
        s = self.s
        s.op("dve", lambda e: e.memset(self.S.rearrange("p h v -> p (h v)"), 0.0), writes=["S"])
        s.op("dve", lambda e: e.memset(self.Sb.rearrange("p h v -> p (h v)"), 0.0), writes=["Sb"])

    def phase1_tile(self, src_rows):
        self.load_xT(src_rows, T, self.xT, "xT")
        self.norm(self.xT, "xT", 0, T, self.hT, "hT")
        self.scratch()
        h = self.hgrn_alloc(T, 64)
        for hg in range(NH // 2):
            self.hgrn_group(hg, T, 64, h, self.hT, True)

    def memkv(self):
        s = self.s
        self.load_xT(self.d_memp, NMEM, self.xT, "xT")
        self.norm(self.xT, "xT", 3, NMEM, self.hT, "hT")
        self.scratch()
        ktok = self.bf(2 * D).rearrange("p (b d) -> p b d", b=2)
        stg = [self.f32(SLABW) for _ in range(2)]
        it = 0
        for wname, dst_bf, okey, dout in (("w_xk", ktok, "ktok", self.o_mk), ("w_xv", self.V, "V", self.o_mv)):
            for cs in range(0, D, SLABW):
                slot = self.next_slab(wname, 0, KD, cs, SLABW)
                slab = self.ring[slot]
                self._acc_i = getattr(self, "_acc_i", 0) + 1
                acc, pskey = self.ps_acc(self._acc_i)
                for mb in range(2):
                    for k in range(KD):
                        s.op("pe", lambda e, o=acc[:, mb, :], l=self.hT[:, k, mb * P:(mb + 1) * P], r=slab[:, k, :], a=(k == 0), b=(k == KD - 1):
                             e.matmul(o, lhsT=l, rhs=r, start=a, stop=b), reads=[("ring", slot), "hT"], writes=[pskey])
                for mb in range(2):
                    sg = stg[it % 2]
                    skey = ("kst", it % 2)
                    s.op("act", lambda e, o=sg, a=acc[:, mb, :]: e.activation(out=o, in_=a, func=AF.Copy), reads=[pskey], writes=[skey])
                    s.op("dve", lambda e, o=dst_bf[:, mb, cs:cs + SLABW], a=acc[:, mb, :]: e.tensor_copy(out=o, in_=a),
                         reads=[pskey], writes=[okey])
                    s.op("sp", lambda e, d=dout[mb * P:(mb + 1) * P, cs:cs + SLABW], a=sg: e.dma_start(out=d, in_=a),
                         reads=[skey, ("xst", 1)], writes=[("out", self._nout())], dma_sem=self.sem_kst[it % 2])
                    it += 1
        self.k_transposes(ktok, "ktok", [("xst", 0)])

    def k_transposes(self, ktok, kkey, extra=()):
        s = self.s
        pbf = self.pbf.rearrange("p (i t) -> p i t", i=8)
        for c in range(KD):
            g = c % 2
            for mb in range(2):
                s.op("pe", lambda e, o=pbf[:, 4 * g + mb, :], a=ktok[:, mb, c * P:(c + 1) * P]:
                     e.transpose(out=o, in_=a, identity=self.ident_b), reads=[kkey, "consts"] + list(extra), writes=[("ps", 4)])
            self.copy(self.evac_engine(), self.KT[:, c, :].rearrange("p (b m) -> p b m", b=2), pbf[:, 4 * g:4 * g + 2, :],
                      reads=[("ps", 4)], writes=["KT"])

    def prompt_tile(self, t):
        n = T
        self.load_xT(self.d_xp[t * T:(t + 1) * T, :], n, self.xT, "xT")
        if t == 0:
            self.load_xT(self.d_xprev, 2, self.xprevT, "xprevT")
        if "mixer" in self.stages:
            self.mixer(n, 64, self.carry, "carry", with_prev=(t == 0))
        if self.debug and t == 0:
            self.dbg_dump(0)
        if "xattn" in self.stages:
            self.xattn(n)
        if self.debug and t == 0:
            self.dbg_dump(1)
        if "ffn" in self.stages:
            self.ffn(n)
        if self.debug and t == 0:
            self.dbg_dump(2)
        self.final_out(n, self.o_yp[t * T:(t + 1) * T, :])

    def dbg_dump(self, i):
        s = self.s
        s.op("sp", lambda e: e.dma_start(out=self.o_dbg[:, i * KD * T:(i + 1) * KD * T], in_=self.xT.rearrange("p k t -> p (k t)")),
             reads=["xT"], writes=[("out", self._nout())], dma_sem=self.sem_misc[i])

    def store_small(self, dst, src, key, semi):
        self.s.op("sp", lambda e: e.dma_start(out=dst, in_=src), reads=[key], writes=[("out", self._nout())], dma_sem=self.sem_misc[semi])

    def sample_pass(self):
        s = self.s
        n = TS
        s.op("sp", lambda e: e.dma_start(out=self.S.rearrange("p h v -> p (h v)"), in_=self.d_sh), writes=["S"], dma_sem=self.sem_misc[4])
        s.op("act", lambda e: e.activation(out=self.Sb.rearrange("p h v -> p (h v)"), in_=self.S.rearrange("p h v -> p (h v)"), func=AF.Copy),
             reads=["S"], writes=["Sb"])
        s.op("sp", lambda e: e.dma_start(out=self.carry_s.rearrange("p c r -> p (c r)"), in_=self.d_sc), writes=["carry_s"], dma_sem=self.sem_misc[5])
        self.load_xT(self.d_xs, n, self.xT, "xT")
        if "mixer" in self.stages:
            self.mixer(n, 32, self.carry_s, "carry_s", with_prev=False)
        self.scratch()
        self.off = self.scratch0 + 20 * 1024
        ktok = self.bf(2 * D).rearrange("p (b d) -> p b d", b=2)
        s.op("pool", lambda e: e.dma_start(out=ktok, in_=self.d_cmk.rearrange("(b p) d -> p b d", p=P)),
             writes=["ktok_s", ("xst", 0), ("xst", 1)], dma_sem=self.sem_misc[6], after_all=True)
        s.op("pool", lambda e: e.dma_start(out=self.V, in_=self.d_cmv.rearrange("(b p) d -> p b d", p=P)),
             writes=["V"], dma_sem=self.sem_misc[7])
        self.k_transposes(ktok, "ktok_s")
        if "xattn" in self.stages:
            self.xattn(n)
        if "ffn" in self.stages:
            self.ffn(n)
        self.final_out(n, self.o_ys)

    def program(self):
        self.outkeys = []
        self._outc = 0
        self._acc_i = 0
        self._alt = 0
        self._ci = -1
        self.setup()
        self.zero_state()
        for t in range(self.NP1 if "p1" in self.stages else 0):
            self.phase1_tile(self.d_xpred[t * T:(t + 1) * T, :])
        s = self.s
        if self.NP1 > 0:
            s.op("act", lambda e: e.activation(out=self.Sb.rearrange("p h v -> p (h v)"), in_=self.S.rearrange("p h v -> p (h v)"), func=AF.Copy),
                 reads=["S"], writes=["Sb"])
        if "memkv" in self.stages:
            self.memkv()
        for t in range(self.NT if "tile" in self.stages else 0):
            self.prompt_tile(t)
        self.store_small(self.o_hp, self.S.rearrange("p h v -> p (h v)"), "S", 0)
        self.store_small(self.o_cp, self.carry.rearrange("p c r -> p (c r)"), "carry", 1)
        if "sample" in self.stages:
            self.sample_pass()
        self.store_small(self.o_hs, self.S.rearrange("p h v -> p (h v)"), "S", 2)
        self.store_small(self.o_cs, self.carry_s.rearrange("p c r -> p (c r)"), "carry_s", 3)
        s.op("sp", lambda e: e.wait_ge(self.sem_misc[0], 0), reads=list(self.outkeys))

    def build(self):
        nc = self.nc
        s = self.s
        s.dry = True
        self.wc_count = {}
        self.program()
        plan = s.plan
        self.wc_off = {}
        CAP = 720 * 1024
        sizes = [0]
        for spec in plan:
            if self.wc_count[spec] > 1 and spec not in self.wc_off:
                n_ = spec[2] * spec[4]
                if sizes[-1] + n_ > CAP:
                    sizes.append(0)
                self.wc_off[spec] = (len(sizes) - 1, sizes[-1])
                sizes[-1] += n_
        self.wc_written = set()
        self.wcaches = [nc.dram_tensor(f"wcache{i}", [P, max(sz, 1)], BF16).ap() for i, sz in enumerate(sizes) if sz]
        s.dry = False
        s.reset()
        s.plan = plan
        self.program()
        assert s.plan_pos == len(plan), (s.plan_pos, len(plan))
        run = s.emit(nc, None, self.sems)
        with nc.Block() as block:
            @block.tensor
            def _(e):
                run("pe", e)

            @block.scalar
            def _(e):
                run("act", e)

            @block.vector
            def _(e):
                run("dve", e)

            @block.gpsimd
            def _(e):
                run("pool", e)

            @block.sync
            def _(e):
                run("sp", e)
        return nc


def _pc(v):
    v = np.asarray(v, dtype=np.float32)
    return np.ascontiguousarray(v.reshape(-1, P).T)


def make_in_maps(inp, NT, wnames):
    SEG = NT * T
    xpr = np.asarray(inp["x_prompt"], dtype=np.float32)
    ident = np.eye(P, dtype=np.float32)
    maskc = np.triu(np.ones((64, 64), dtype=np.float32))
    gains = np.stack([_pc(inp["norm_mix"][0]), _pc(inp["norm_xattn"][0]), _pc(inp["norm_ffn"][0]),
                      _pc(inp["norm_mem"][0]), _pc(inp["norm_final"])], axis=1).reshape(P, 5 * KD)
    lbl = np.ascontiguousarray(np.asarray(inp["lb_logits"], np.float32).reshape(2, NH, P).transpose(2, 0, 1)).reshape(P, 2 * NH)
    hgn = _pc(inp["hg_norm"][0])
    cw = np.ascontiguousarray(np.asarray(inp["conv_w"][0], np.float32).reshape(3, NH, P).transpose(2, 0, 1)).reshape(P, 3 * NH)
    shared = {"ident": ident, "maskc": maskc, "gains": np.ascontiguousarray(gains), "lbl": lbl, "hgn": hgn, "cw": cw}
    for n in wnames:
        shared[n] = np.ascontiguousarray(np.asarray(inp[n][0], dtype=np.float32))
    maps = []
    for c in range(8):
        b, j = c // 4, c % 4
        m = dict(shared)
        m["xp"] = np.ascontiguousarray(xpr[b, j * SEG:(j + 1) * SEG])
        xpred = np.zeros((3 * SEG, D), np.float32)
        if j > 0:
            xpred[(3 - j) * SEG:] = xpr[b, 0:j * SEG]
        m["xpred"] = xpred
        xprev = np.zeros((2, D), np.float32)
        if j > 0:
            xprev[:] = xpr[b, j * SEG - 2:j * SEG]
        m["xprev"] = xprev
        m["xs"] = np.ascontiguousarray(np.asarray(inp["x_sample"][c], np.float32))
        m["cmk"] = np.ascontiguousarray(np.asarray(inp["cache_mem_k"][0, c], np.float32).reshape(NMEM, D))
        m["cmv"] = np.ascontiguousarray(np.asarray(inp["cache_mem_v"][0, c], np.float32).reshape(NMEM, D))
        m["sh"] = np.ascontiguousarray(np.asarray(inp["state_hgrn"][0, c], np.float32).transpose(1, 0, 2)).reshape(P, NH * 128)
        m["sc"] = np.ascontiguousarray(np.asarray(inp["state_conv"][0, c], np.float32).reshape(2, NH, P).transpose(2, 1, 0)).reshape(P, 32)
        m["memp"] = np.ascontiguousarray(np.asarray(inp["mem_prompt"][b], np.float32))
        maps.append(m)
    return maps


def assemble(results, NT):
    SEG = NT * T
    L = 4 * SEG
    y_prompt = np.zeros((2, L, D), np.float32)
    y_sample = np.zeros((8, TS, D), np.float32)
    mk = np.zeros((1, 2, NMEM, XH, D // XH), np.float32)
    mv = np.zeros((1, 2, NMEM, XH, D // XH), np.float32)
    hp = np.zeros((1, 2, NH, 128, 128), np.float32)
    cp = np.zeros((1, 2, 2, HW), np.float32)
    hs = np.zeros((1, 8, NH, 128, 128), np.float32)
    cs = np.zeros((1, 8, 2, HW), np.float32)
    st = lambda a: np.asarray(a).reshape(P, NH, 128).transpose(1, 0, 2)
    cv = lambda a: np.asarray(a).reshape(P, NH, 2).transpose(2, 1, 0).reshape(2, HW)
    for c in range(8):
        b, j = c // 4, c % 4
        r = results[c]
        y_prompt[b, j * SEG:(j + 1) * SEG] = r["yp"]
        y_sample[c] = r["ys"]
        hs[0, c] = st(r["hs"])
        cs[0, c] = cv(r["cs"])
        if j == 0:
            mk[0, b] = np.asarray(r["mk"]).reshape(NMEM, XH, D // XH)
            mv[0, b] = np.asarray(r["mv"]).reshape(NMEM, XH, D // XH)
        if j == 3:
            hp[0, b] = st(r["hp"])
            cp[0, b] = cv(r["cp"])
    return (y_prompt, y_sample, mk, mv, hp, cp, hs, cs)


def run(inp, NT, debug=False, stages=None):
    import time, sys
    t0 = time.time()
    bld = Builder(NT, 3 * NT, debug=debug, stages=stages)
    nc = bld.build()
    t1 = time.time()
    maps = make_in_maps(inp, NT, list(bld.w.keys()))
    t2 = time.time()
    res = run_bass_kernel_spmd(nc, maps, core_ids=list(range(8)))
    print(f"[kernel] build {t1 - t0:.1f}s maps {t2 - t1:.1f}s run {time.time() - t2:.1f}s nops={ {e: len(v) for e, v in bld.s.ops.items()} }", file=sys.stderr)
    outs = assemble(res.results, NT)
    if debug:
        return outs, [r["dbg"] for r in res.results]
    return outs


def kernel(**inputs):
    return run(inputs, 8)
```

```python
import numpy as np
import concourse.bass as bass
import concourse.mybir as mybir
from concourse.bass_utils import run_bass_kernel_spmd

F32 = mybir.dt.float32
BF16 = mybir.dt.bfloat16
AF = mybir.ActivationFunctionType
ALU = mybir.AluOpType
AX = mybir.AxisListType

P = 128
D = 4096
KD = 32
HW = 2048
NH = 16
FF = 11008
KF = 86
NMEM = 256
XH = 4
PROJ = 22528
EPS = 1e-6
T = 256
TS = 32
C_Q, C_F, C_IV, C_OG, C_CH, C_CB, C_CC, C_GA, C_GB = 0, 2048, 4096, 6144, 8192, 10240, 12288, 14336, 18432
NRING = 3
SLABW = 256


class Op:
    __slots__ = ("eng", "fn", "deps", "signal", "ev", "dma_sem", "idx", "inc")

    def __init__(self, eng, fn, dma_sem, inc):
        self.eng = eng
        self.fn = fn
        self.deps = []
        self.signal = False
        self.ev = None
        self.dma_sem = dma_sem
        self.inc = inc


class Sched:
    ENGS = ("pe", "act", "dve", "pool", "sp")

    def __init__(self):
        self.dry = False
        self.reset()

    def reset(self):
        self.ops = {e: [] for e in self.ENGS}
        self.last_writer = {}
        self.readers = {}
        self.n = 0
        self.plan = []
        self.plan_pos = 0
        self.issued = 0

    def op(self, eng, fn, reads=(), writes=(), dma_sem=None, inc=16, after_all=False):
        if self.dry:
            return None
        o = Op(eng, fn, dma_sem, inc)
        o.idx = self.n
        self.n += 1
        psr = [k for k in reads if isinstance(k, tuple) and k[0] == "ps"]
        if psr:
            reads = [k for k in reads if not (isinstance(k, tuple) and k[0] == "ps")]
            writes = list(writes) + psr
        deps = {}
        if after_all:
            for e_ in ("pe", "act", "dve"):
                if self.ops[e_]:
                    w = self.ops[e_][-1]
                    deps[id(w)] = w
        lw = self.last_writer
        rd = self.readers
        for k in reads:
            w = lw.get(k)
            if w is not None:
                deps[id(w)] = w
        for k in writes:
            w = lw.get(k)
            if w is not None:
                deps[id(w)] = w
            for r in rd.get(k, ()):
                deps[id(r)] = r
        best = {}
        out = []
        for d in deps.values():
            if d is o:
                continue
            if d.dma_sem is not None:
                out.append(d)
            else:
                b = best.get(d.eng)
                if b is None or d.idx > b.idx:
                    best[d.eng] = d
        for e, d in best.items():
            if e == "pe" and eng == "pe" and dma_sem is None:
                continue
            out.append(d)
        for d in out:
            d.signal = True
        o.deps = out
        for k in writes:
            lw[k] = o
            rd[k] = []
        for k in reads:
            rd.setdefault(k, []).append(o)
        self.ops[eng].append(o)
        return o

    def fence(self, old_keys, new_keys):
        if self.dry:
            return
        prev = []
        for k in old_keys:
            w = self.last_writer.get(k)
            if w is not None:
                prev.append(w)
            prev.extend(self.readers.get(k, ()))
        for k in new_keys:
            self.last_writer.pop(k, None)
            self.readers[k] = list(prev)

    def emit(self, nc, engines, sems):
        cnt = {}
        for e in self.ENGS:
            c = 0
            for o in self.ops[e]:
                if o.dma_sem is not None:
                    v = cnt.get(o.dma_sem, 0) + o.inc
                    cnt[o.dma_sem] = v
                    o.ev = (o.dma_sem, v)
                elif o.signal:
                    c += 1
                    o.ev = (sems[e], c)

        def run(ename, eng):
            waited = {}
            for o in self.ops[ename]:
                for d in o.deps:
                    s, v = d.ev
                    key = id(s)
                    if waited.get(key, 0) < v:
                        eng.wait_ge(s, v)
                        waited[key] = v
                ins = o.fn(eng)
                if o.dma_sem is not None:
                    ins.then_inc(o.dma_sem, o.inc)
                elif o.signal:
                    ins.then_inc(sems[ename], 1)
        return run


class Builder:
    def __init__(self, NT, NP1, debug=False, stages=None):
        self.stages = stages if stages is not None else {"p1", "memkv", "tile", "sample", "mixer", "xattn", "ffn"}
        self.NT = NT
        self.NP1 = NP1
        self.SEG = NT * T
        self.debug = debug
        nc = bass.Bass("TRN2", target_bir_lowering=False)
        self.nc = nc
        self.s = Sched()
        self._declare_dram()
        self._alloc()

    def _declare_dram(self):
        nc = self.nc
        di = lambda n, sh: nc.dram_tensor(n, sh, F32, kind="ExternalInput").ap()
        do = lambda n, sh: nc.dram_tensor(n, sh, F32, kind="ExternalOutput").ap()
        SEG = self.SEG
        self.d_xp = di("xp", [SEG, D])
        self.d_xpred = di("xpred", [max(self.NP1, 1) * T, D])
        self.d_xprev = di("xprev", [2, D])
        self.d_xs = di("xs", [TS, D])
        self.d_cmk = di("cmk", [NMEM, D])
        self.d_cmv = di("cmv", [NMEM, D])
        self.d_sh = di("sh", [P, NH * 128])
        self.d_sc = di("sc", [P, 32])
        self.d_memp = di("memp", [NMEM, D])
        self.d_ident = di("ident", [P, P])
        self.d_mask = di("maskc", [64, 64])
        self.d_gains = di("gains", [P, 5 * KD])
        self.d_lbl = di("lbl", [P, 2 * NH])
        self.d_hgn = di("hgn", [P, NH])
        self.d_cw = di("cw", [P, 3 * NH])
        self.w = {}
        self.wshapes = dict((("w_in", [D, PROJ]), ("w_a", [HW, D]), ("w_b", [HW, D]), ("w_o", [D, D]),
                             ("w_xq", [D, D]), ("w_xk", [D, D]), ("w_xv", [D, D]), ("w_xo", [D, D]),
                             ("w_gate", [D, FF]), ("w_up", [D, FF]), ("w_down", [FF, D])))
        self.o_yp = do("yp", [SEG, D])
        self.o_ys = do("ys", [TS, D])
        self.o_mk = do("mk", [NMEM, D])
        self.o_mv = do("mv", [NMEM, D])
        self.o_hp = do("hp", [P, NH * 128])
        self.o_cp = do("cp", [P, 32])
        self.o_hs = do("hs", [P, NH * 128])
        self.o_cs = do("cs", [P, 32])
        if self.debug:
            self.o_dbg = do("dbg", [P, 4 * KD * T])

    def _alloc(self):
        nc = self.nc
        ARENA = 212800 // 4
        arena = nc.alloc_sbuf_tensor("arena", [P, ARENA], F32)
        self.A = arena.ap()
        self.Ab = self.A.bitcast(BF16)
        self.off = 0

        def f32(nbytes_elems, shape=None):
            n = nbytes_elems
            a = self.off // 4
            self.off += n * 4
            ap = self.A[:, a:a + n]
            return ap

        def bf(n):
            n2 = (n + 1) // 2 * 2
            a = self.off // 2
            self.off += n2 * 2
            return self.Ab[:, a:a + n]

        self.f32 = f32
        self.bf = bf
        r3 = lambda ap, k: ap.rearrange("p (k t) -> p k t", k=k)
        self.xT = r3(f32(KD * T), KD)
        self.hT = r3(bf(KD * T), KD)
        self.S = r3(f32(NH * 128), NH)
        self.Sb = r3(bf(NH * 128), NH)
        self.KT = r3(bf(KD * NMEM), KD)
        self.V = r3(bf(2 * D), 2)
        self.ring = [r3(bf(KD * SLABW), KD) for _ in range(NRING)]
        self.ident_f = f32(128)
        self.ident_b = bf(128)
        self.ones_b = bf(128)
        self.mask = f32(64)
        self.gains = r3(f32(5 * KD), 5)
        self.lbl = r3(f32(2 * NH), 2)
        self.lb = f32(NH)
        self.oml = f32(NH)
        self.hgn = f32(NH)
        self.cw = r3(f32(3 * NH), 3)
        self.rmask = f32(2 * T)
        self.rmask32 = f32(2 * TS)
        self.carry = r3(f32(NH * 2), NH)
        self.carry_s = r3(f32(NH * 2), NH)
        self.chp = r3(f32(4), 2)
        self.xprevT = r3(f32(KD * 2), KD)
        self.hprevT = r3(bf(KD * 2), KD)
        self.tiny = f32(16)
        self.sq = [r3(bf(4 * T), 4) for _ in range(2)]
        self.r0 = f32(2 * T)
        self.R = f32(2 * T)
        self.scratch0 = self.off
        self.scratch_bytes = ARENA * 4 - self.off
        assert self.scratch_bytes >= 53700, self.scratch_bytes
        self.psb = []
        for i in range(8):
            if i == 4:
                self.pbf = nc.alloc_psum_tensor("pbf", [P, 1024], BF16).ap()
                self.psb.append(None)
            else:
                self.psb.append(nc.alloc_psum_tensor(f"psb{i}", [P, 512], F32).ap())
        sem = lambda n: nc.alloc_semaphore(n)
        self.sems = {e: sem("e_" + e) for e in Sched.ENGS}
        self.sem_ring = [sem(f"ring{i}") for i in range(NRING)]
        self.sem_xst = [sem(f"xst{i}") for i in range(2)]
        self.sem_wb = [sem(f"wb{i}") for i in range(NRING)]
        self.sem_cv = [sem(f"cv{i}") for i in range(8)]
        self.sem_kst = [sem(f"kst{i}") for i in range(2)]
        self.sem_c = [sem(f"c{i}") for i in range(6)]
        self.sem_misc = [sem(f"m{i}") for i in range(8)]

    def scratch(self):
        self.off = self.scratch0

    def next_slab(self, wname, k0, nk, c0, ncols):
        s = self.s
        spec = (wname, k0, nk, c0, ncols)
        if s.dry:
            if wname not in self.w:
                self.w[wname] = self.nc.dram_tensor(wname, self.wshapes[wname], F32, kind="ExternalInput").ap()
            s.plan.append(spec)
            self.wc_count[spec] = self.wc_count.get(spec, 0) + 1
            return 0
        i = s.plan_pos
        assert s.plan[i] == spec, (s.plan[i], spec)
        s.plan_pos += 1
        while s.issued < len(s.plan) and s.issued <= i + NRING - 1:
            self._issue_slab(s.issued)
            s.issued += 1
        return i % NRING

    def _issue_slab(self, i):
        spec = self.s.plan[i]
        wname, k0, nk, c0, ncols = spec
        slot = i % NRING
        dst = self.ring[slot][:, 0:nk, 0:ncols]
        off = self.wc_off.get(spec)
        if off is not None and spec in self.wc_pre:
            src = self.wcaches[off[0]][:, off[1]:off[1] + nk * ncols].rearrange("p (k n) -> p k n", k=nk)
            self.s.op("pool", lambda e, d=dst, s_=src: e.dma_start(out=d, in_=s_),
                      writes=[("ring", slot)], dma_sem=self.sem_ring[slot])
            return
        if off is not None and spec in self.wc_written:
            src = self.wcaches[off[0]][:, off[1]:off[1] + nk * ncols].rearrange("p (k n) -> p k n", k=nk)
            self.s.op("pool", lambda e, d=dst, s_=src: e.dma_start(out=d, in_=s_),
                      reads=[("wc", off)], writes=[("ring", slot)], dma_sem=self.sem_ring[slot])
            return
        src = self.w[wname][k0 * P:(k0 + nk) * P, c0:c0 + ncols].rearrange("(k p) n -> p k n", p=P)
        self.s.op("pool", lambda e, d=dst, s_=src: e.dma_start(out=d, in_=s_),
                  writes=[("ring", slot)], dma_sem=self.sem_ring[slot])
        if off is not None:
            wdst = self.wcaches[off[0]][:, off[1]:off[1] + nk * ncols].rearrange("p (k n) -> p k n", k=nk)
            self.s.op("sp", lambda e, d=wdst, s_=dst: e.dma_start(out=d, in_=s_),
                      reads=[("ring", slot)], writes=[("wc", off)], dma_sem=self.sem_wb[slot])
            self.wc_written.add(spec)
        if self.in_p1:
            self.p1_loads_left -= 1
            nconv = -(-len(self.conv_todo) // max(self.p1_loads_left, 1)) if self.conv_todo else 0
            for _ in range(min(nconv, len(self.conv_todo))):
                self._issue_conv(self.conv_todo.pop(0))

    def _issue_conv(self, spec):
        wname, k0, nk, c0, ncols = spec
        off = self.wc_off[spec]
        i = self.conv_n % len(self.sem_cv)
        self.conv_n += 1
        wdst = self.wcaches[off[0]][:, off[1]:off[1] + nk * ncols].rearrange("p (k n) -> p k n", k=nk)
        src = self.w[wname][k0 * P:(k0 + nk) * P, c0:c0 + ncols].rearrange("(k p) n -> p k n", p=P)
        self.s.op("pool", lambda e, d=wdst, s_=src: e.dma_start(out=d, in_=s_), writes=[("cv", i)], dma_sem=self.sem_cv[i])
        self.wc_pre.add(spec)

    def conv_fence(self):
        while self.conv_todo:
            self._issue_conv(self.conv_todo.pop(0))
        if self.conv_n:
            self.s.op("pool", lambda e: e.wait_ge(self.sem_misc[0], 0), reads=[("cv", i) for i in range(len(self.sem_cv))])

    def evac_engine(self):
        self._alt = 1 - getattr(self, "_alt", 0)
        return "act" if self._alt else "dve"

    def copy(self, eng, out, in_, reads, writes):
        if eng == "act":
            self.s.op("act", lambda e: e.activation(out=out, in_=in_, func=AF.Copy), reads=reads, writes=writes)
        else:
            self.s.op("dve", lambda e: e.tensor_copy(out=out, in_=in_), reads=reads, writes=writes)

    def ps_acc(self, i):
        b = i % 4
        return self.psb[b].rearrange("p (j t) -> p j t", j=2), ("ps", b)

    def load_xT(self, src_rows, n, xT, xkey):
        s = self.s
        self.scratch()
        st = [self.f32(D) for _ in range(2)]
        nblk = (n + P - 1) // P
        for blk in range(nblk):
            rows = min(P, n - blk * P)
            stg = st[blk % 2]
            skey = ("xst", blk % 2)
            src = src_rows[blk * P:blk * P + rows, :]
            s.op("sp", lambda e, d=stg[0:rows, :], s_=src: e.dma_start(out=d, in_=s_),
                 writes=[skey], dma_sem=self.sem_xst[blk % 2], after_all=True)
            for cg in range(8):
                bank = 5 + (cg % 2)
                tp = self.psb[bank].rearrange("p (i t) -> p i t", i=4)
                for i in range(4):
                    c = 4 * cg + i
                    s.op("pe", lambda e, o=tp[:, i, 0:rows], a=stg[0:rows, c * P:(c + 1) * P], r=rows:
                         e.transpose(out=o, in_=a, identity=self.ident_f[0:r, 0:r]),
                         reads=[skey, "consts"], writes=[("ps", bank)])
                self.copy(self.evac_engine(), xT[:, 4 * cg:4 * cg + 4, blk * P:blk * P + rows], tp[:, :, 0:rows],
                          reads=[("ps", bank)], writes=[xkey])

    def norm(self, xT, xkey, gidx, n, out, okey, out_scale=None):
        s = self.s
        ssq = self.psb[6][:, 0:n]
        for grp in range(8):
            sq = self.sq[grp % 2]
            s.op("act", lambda e, o=sq[:, :, 0:n], a=xT[:, 4 * grp:4 * grp + 4, 0:n]:
                 e.activation(out=o, in_=a, func=AF.Square), reads=[xkey], writes=[("sq", grp % 2)])
            for i in range(4):
                s.op("pe", lambda e, a=sq[:, i, 0:n], st=(grp == 0 and i == 0), sp=(grp == 7 and i == 3):
                     e.matmul(ssq, lhsT=self.ones_b, rhs=a, start=st, stop=sp),
                     reads=[("sq", grp % 2), "consts"], writes=[("ps", 6)])
        r0 = self.r0[:, 0:n]
        R = self.R[:, 0:n]
        s.op("act", lambda e: e.activation(out=r0, in_=ssq, func=AF.Sqrt, bias=EPS, scale=1.0 / D),
             reads=[("ps", 6)], writes=["r0"])
        s.op("dve", lambda e: e.reciprocal(out=R, in_=r0), reads=["r0"], writes=["R"])
        for k in range(KD):
            s.op("dve", lambda e, o=out[:, k, 0:n], a=xT[:, k, 0:n], g=self.gains[:, gidx, k:k + 1]:
                 e.scalar_tensor_tensor(out=o, in0=a, scalar=g, in1=R, op0=ALU.mult, op1=ALU.mult),
                 reads=[xkey, "R", "consts"], writes=[okey])

    def proj(self, wname, k0, nk, c0, ncols, rhs_list, evac):
        s = self.s
        pieces = []
        kk = 0
        while kk < nk:
            m = min(KD, nk - kk)
            pieces.append((kk, m))
            kk += m
        for cs in range(0, ncols, SLABW):
            w_ = min(SLABW, ncols - cs)
            nj = w_ // P
            self._acc_i = getattr(self, "_acc_i", 0) + 1
            acc, pskey = self.ps_acc(self._acc_i)
            accs = [(acc[:, j, :], pskey) for j in range(2)]
            if len(pieces) > 1:
                self._acc_i += 1
                acc2, pskey2 = self.ps_acc(self._acc_i)
                accs = [(acc[:, 0, :], pskey), (acc2[:, 0, :], pskey2)]
            for pi, (pk, pm) in enumerate(pieces):
                slot = self.next_slab(wname, k0 + pk, pm, c0 + cs, w_)
                slab = self.ring[slot]
                for j in range(nj):
                    for k in range(pm):
                        first = (pi == 0 and k == 0)
                        last = (pi == len(pieces) - 1 and k == pm - 1)
                        for g, (rhs_fn, n, rkeys) in enumerate(rhs_list):
                            if g == 0:
                                o = accs[j][0][:, 0:n]
                                wk = [accs[j][1]]
                            else:
                                o = self.psb[7][:, (j * 4 + g) * 32:(j * 4 + g) * 32 + n]
                                wk = [("ps", 7)]
                            s.op("pe", lambda e, o=o, l=slab[:, k, j * P:(j + 1) * P], r=rhs_fn(pk + k), st=first, sp=last:
                                 e.matmul(o, lhsT=l, rhs=r, start=st, stop=sp),
                                 reads=[("ring", slot)] + list(rkeys), writes=wk)
            for j in range(nj):
                oc = (c0 + cs) // P + j
                for g, (rhs_fn, n, rkeys) in enumerate(rhs_list):
                    if g == 0:
                        evac(g, oc, j, accs[j][0][:, 0:n], accs[j][1])
                    else:
                        evac(g, oc, j, self.psb[7][:, (j * 4 + g) * 32:(j * 4 + g) * 32 + n], ("ps", 7))

    def hgrn_alloc(self, n, C):
        nch = n // C
        r2 = lambda ap: ap.rearrange("p (j t) -> p j t", j=2)
        h = {}
        h["Q"] = r2(self.f32(2 * n)); h["G"] = r2(self.f32(2 * n)); h["KK"] = r2(self.f32(2 * n)); h["D1"] = r2(self.f32(2 * n))
        h["B"] = r2(self.f32(2 * n))
        for nm in ("VT", "OGS", "QT", "KT", "KH", "SQ"):
            h[nm] = r2(self.bf(2 * n))
        h["ktv"] = self.bf(nch * 4 * 128).rearrange("p (c i v) -> p c i v", c=nch, i=4)
        h["scm"] = r2(self.bf(2 * C))
        h["DD"] = self.f32(2 * nch).rearrange("p (j c) -> p j c", j=2)
        h["bsum"] = self.f32(2)
        return h

    def hgrn_group(self, hg, n, C, h, hT, state_only, oN=None):
        s = self.s
        nch = n // C
        h0 = 2 * hg
        rhs = [(lambda k: hT[:, k, 0:n], n, ["hT"])]
        Q, G, KK, D1, B = h["Q"], h["G"], h["KK"], h["D1"], h["B"]
        VT, OGS, QT, KT, KH, SQ = h["VT"], h["OGS"], h["QT"], h["KT"], h["KH"], h["SQ"]
        ktv, scm, DD = h["ktv"], h["scm"], h["DD"]

        def ev_act(dst, key, func):
            def f(g, oc, j, ps, pskey):
                s.op("act", lambda e: e.activation(out=dst[:, j, :], in_=ps, func=func), reads=[pskey], writes=[key])
            return f

        def ev_dve(dst, key):
            def f(g, oc, j, ps, pskey):
                s.op("dve", lambda e: e.tensor_copy(out=dst[:, j, :], in_=ps), reads=[pskey], writes=[key])
            return f

        if not state_only:
            self.proj("w_in", 0, KD, C_Q + hg * 256, 256, rhs, ev_act(Q, "hQ", AF.Copy))
        self.proj("w_in", 0, KD, C_F + hg * 256, 256, rhs, ev_act(G, "hG", AF.Sigmoid))
        self.proj("w_in", 0, KD, C_IV + hg * 256, 256, rhs, ev_dve(VT, "hVT"))
        if not state_only:
            self.proj("w_in", 0, KD, C_OG + hg * 256, 256, rhs, ev_act(OGS, "hOGS", AF.Silu))
        for j in range(2):
            hh = h0 + j
            s.op("dve", lambda e, a=G[:, j, :], m=self.oml[:, hh:hh + 1], b=self.lb[:, hh:hh + 1]:
                 e.tensor_scalar(out=a, in0=a, scalar1=m, scalar2=b, op0=ALU.mult, op1=ALU.add),
                 reads=["hG", "consts"], writes=["hG"])
        s.op("dve", lambda e: e.tensor_scalar(out=KK, in0=G, scalar1=-1.0, scalar2=1.0, op0=ALU.mult, op1=ALU.add),
             reads=["hG"], writes=["hKK"])
        s.op("act", lambda e: e.activation(out=G, in_=G, func=AF.Ln), reads=["hG"], writes=["hG"])
        Gf = G.rearrange("p j t -> p (j t)")
        Bf = B.rearrange("p j t -> p (j t)")
        rm = self.rmask_C[C][:, 0:2 * n]
        s.op("dve", lambda e: e.tensor_tensor_scan(out=Bf, data0=rm, data1=Gf, initial=0.0, op0=ALU.mult, op1=ALU.add),
             reads=["hG", "consts"], writes=["hB"])
        B4 = B.rearrange("p j (c t) -> p j c t", c=nch)
        bl = B4[:, :, :, C - 1:C]
        if not state_only:
            s.op("act", lambda e: e.activation(out=D1, in_=B, func=AF.Exp), reads=["hB"], writes=["hD1"])
            s.op("dve", lambda e: e.tensor_tensor(out=QT, in0=Q, in1=D1, op=ALU.mult), reads=["hQ", "hD1"], writes=["hQT"])
            s.op("act", lambda e: e.activation(out=D1, in_=B, func=AF.Exp, scale=-1.0), reads=["hB", "hQT"], writes=["hD1"])
            s.op("dve", lambda e: e.tensor_tensor(out=KT, in0=KK, in1=D1, op=ALU.mult), reads=["hKK", "hD1"], writes=["hKT"])
        D14 = D1.rearrange("p j (c t) -> p j c t", c=nch)
        s.op("dve", lambda e: e.tensor_tensor(out=D14, in0=bl.broadcast_to([P, 2, nch, C]), in1=B4, op=ALU.subtract),
             reads=["hB", "hKT"], writes=["hD1"])
        s.op("act", lambda e: e.activation(out=D1, in_=D1, func=AF.Exp), reads=["hD1"], writes=["hD1"])
        s.op("dve", lambda e: e.tensor_tensor(out=KH, in0=KK, in1=D1, op=ALU.mult), reads=["hKK", "hD1"], writes=["hKH"])
        s.op("act", lambda e: e.activation(out=DD, in_=B4[:, :, :, C - 1], func=AF.Exp),
             reads=["hB"], writes=["hDD"])
        tpk = self.pbf[:, 0:512].rearrange("p (i v) -> p i v", i=4)
        for c in range(nch):
            for j in range(2):
                s.op("pe", lambda e, o=tpk[0:C, j, :], a=KH[:, j, c * C:(c + 1) * C]:
                     e.transpose(out=o, in_=a, identity=self.ident_b), reads=["hKH", "consts"], writes=[("ps", 4)])
                s.op("pe", lambda e, o=tpk[0:C, 2 + j, :], a=VT[:, j, c * C:(c + 1) * C]:
                     e.transpose(out=o, in_=a, identity=self.ident_b), reads=["hVT", "consts"], writes=[("ps", 4)])
            self.copy(self.evac_engine(), ktv[0:C, c, :, :], tpk[0:C, :, :], reads=[("ps", 4)], writes=["hktv"])
        OT = Q
        sc = self.psb[5][:, 0:2 * C].rearrange("p (j t) -> p j t", j=2)
        po = self.psb[6][:, 0:2 * C].rearrange("p (j t) -> p j t", j=2)
        pst = self.psb[7][:, 0:256].rearrange("p (j v) -> p j v", j=2)
        for c in range(nch):
            cs = slice(c * C, (c + 1) * C)
            if not state_only:
                for j in range(2):
                    s.op("pe", lambda e, o=sc[0:C, j, :], l=KT[:, j, cs], r=QT[:, j, cs]:
                         e.matmul(o, lhsT=l, rhs=r, start=True, stop=True), reads=["hKT", "hQT"], writes=[("ps", 5)])
                s.op("dve", lambda e: e.tensor_tensor(out=scm[0:C, :, :], in0=sc[0:C, :, :],
                                                      in1=self.mask_C[C].unsqueeze(1).broadcast_to([C, 2, C]), op=ALU.mult),
                     reads=[("ps", 5), "consts"], writes=["hscm"])
                for j in range(2):
                    s.op("pe", lambda e, o=po[:, j, :], l=self.Sb[:, h0 + j, :], r=QT[:, j, cs]:
                         e.matmul(o, lhsT=l, rhs=r, start=True, stop=False), reads=["Sb", "hQT"], writes=[("ps", 6)])
                    s.op("pe", lambda e, o=po[:, j, :], l=ktv[0:C, c, 2 + j, :], r=scm[0:C, j, :]:
                         e.matmul(o, lhsT=l, rhs=r, start=False, stop=True), reads=["hktv", "hscm"], writes=[("ps", 6)])
                s.op("act", lambda e, o=OT[:, :, cs]: e.activation(out=o, in_=po, func=AF.Copy),
                     reads=[("ps", 6), "hQT"], writes=["hQ"])
            for j in range(2):
                s.op("pe", lambda e, o=pst[:, j, :], l=ktv[0:C, c, j, :], r=ktv[0:C, c, 2 + j, :]:
                     e.matmul(o, lhsT=l, rhs=r, start=True, stop=True), reads=["hktv"], writes=[("ps", 7)])
            for j in range(2):
                s.op("dve", lambda e, a=self.S[:, h0 + j, :], d=DD[:, j, c:c + 1], p_=pst[:, j, :]:
                     e.scalar_tensor_tensor(out=a, in0=a, scalar=d, in1=p_, op0=ALU.mult, op1=ALU.add),
                     reads=[("ps", 7), "hDD", "S"], writes=["S"])
            if not state_only:
                s.op("act", lambda e: e.activation(out=self.Sb[:, h0:h0 + 2, :], in_=self.S[:, h0:h0 + 2, :], func=AF.Copy),
                     reads=["S"], writes=["Sb"])
        if state_only:
            return
        s.op("act", lambda e: e.activation(out=SQ, in_=OT, func=AF.Square), reads=["hQ"], writes=["hSQ"])
        ssq = self.psb[5][:, 0:2 * n].rearrange("p (j t) -> p j t", j=2)
        for j in range(2):
            s.op("pe", lambda e, o=ssq[:, j, :], a=SQ[:, j, :]: e.matmul(o, lhsT=self.ones_b, rhs=a, start=True, stop=True),
                 reads=["hSQ", "consts"], writes=[("ps", 5)])
        s.op("act", lambda e: e.activation(out=D1, in_=ssq, func=AF.Sqrt, bias=EPS, scale=1.0 / 128),
             reads=[("ps", 5)], writes=["hD1"])
        s.op("dve", lambda e: e.reciprocal(out=D1, in_=D1), reads=["hD1"], writes=["hD1"])
        for j in range(2):
            hh = h0 + j
            s.op("dve", lambda e, a=OT[:, j, :], g=self.hgn[:, hh:hh + 1], r=D1[:, j, :]:
                 e.scalar_tensor_tensor(out=a, in0=a, scalar=g, in1=r, op0=ALU.mult, op1=ALU.mult),
                 reads=["hQ", "hD1", "consts"], writes=["hQ"])
        s.op("dve", lambda e: e.tensor_tensor(out=oN[:, h0:h0 + 2, 0:n], in0=OT, in1=OGS, op=ALU.mult),
             reads=["hQ", "hOGS"], writes=["oN"])

    def mixer(self, n, C, carry, ckey, with_prev):
        s = self.s
        xT, hT = self.xT, self.hT
        self.norm(xT, "xT", 0, n, hT, "hT")
        if with_prev:
            self.norm(self.xprevT, "xprevT", 0, 2, self.hprevT, "hprevT")
        self.scratch()
        r3 = lambda ap, k: ap.rearrange("p (k t) -> p k t", k=k)
        oN = r3(self.bf(NH * n), NH)
        cbz = r3(self.bf(NH * n), NH)
        merged = r3(self.bf(KD * n), KD)
        grp0 = self.off
        h = self.hgrn_alloc(n, C)
        s.fence(["mSGA", "mSGB", "mTA", "mTB", "cCH", "cU"], ["hQ", "hG", "hKK", "hD1", "hB", "hVT", "hOGS", "hQT", "hKT", "hKH", "hSQ", "hktv", "hscm", "hDD"])
        for hg in range(NH // 2):
            self.hgrn_group(hg, n, C, h, hT, False, oN)
        self.off = grp0
        s.fence(["hQ", "hG", "hKK", "hD1", "hB", "hVT", "hOGS", "hQT", "hKT", "hKH", "hSQ", "hktv", "hscm", "hDD"], ["cCH", "cU"])
        r2 = lambda ap: ap.rearrange("p (j t) -> p j t", j=2)
        CH = r2(self.f32(2 * n))
        U = r2(self.f32(2 * (n + 2)))
        rhs = [(lambda k: hT[:, k, 0:n], n, ["hT"])]
        if with_prev:
            rhs.append((lambda k: self.hprevT[:, k, 0:2], 2, ["hprevT"]))
        for cp in range(NH // 2):
            def ev_ch(g, oc, j, ps, pskey):
                if g == 0:
                    s.op("act", lambda e: e.activation(out=CH[:, j, :], in_=ps, func=AF.Copy), reads=[pskey], writes=["cCH"])
                else:
                    s.op("act", lambda e: e.activation(out=self.chp[:, j, :], in_=ps, func=AF.Copy), reads=[pskey], writes=["chp"])

            def ev_cc(g, oc, j, ps, pskey):
                cidx = 2 * cp + j
                if g == 0:
                    s.op("dve", lambda e: e.tensor_tensor(out=U[:, j, 2:n + 2], in0=ps, in1=CH[:, j, :], op=ALU.mult),
                         reads=[pskey, "cCH"], writes=["cU"])
                else:
                    s.op("dve", lambda e: e.tensor_tensor(out=carry[:, cidx, :], in0=ps, in1=self.chp[:, j, :], op=ALU.mult),
                         reads=[pskey, "chp"], writes=[ckey])

            def ev_cb(g, oc, j, ps, pskey):
                cidx = 2 * cp + j
                s.op("dve", lambda e: e.tensor_tensor(out=cbz[:, cidx, 0:n], in0=ps, in1=CH[:, j, :], op=ALU.mult),
                     reads=[pskey, "cCH"], writes=["cbz"])

            self.proj("w_in", 0, KD, C_CH + cp * 256, 256, rhs, ev_ch)
            self.proj("w_in", 0, KD, C_CC + cp * 256, 256, rhs, ev_cc)
            for j in range(2):
                cidx = 2 * cp + j
                s.op("dve", lambda e, j=j, cidx=cidx: e.tensor_copy(out=U[:, j, 0:2], in_=carry[:, cidx, :]),
                     reads=[ckey], writes=["cU"])
                for tap in range(3):
                    wv = self.cw[:, tap, cidx:cidx + 1]
                    if tap == 0:
                        s.op("dve", lambda e, j=j, wv=wv: e.tensor_scalar(out=CH[:, j, :], in0=U[:, j, 0:n], scalar1=wv,
                                                                          scalar2=None, op0=ALU.mult),
                             reads=["cU", "consts"], writes=["cCH"])
                    else:
                        s.op("dve", lambda e, j=j, wv=wv, tap=tap: e.scalar_tensor_tensor(
                            out=CH[:, j, :], in0=U[:, j, tap:tap + n], scalar=wv, in1=CH[:, j, :], op0=ALU.mult, op1=ALU.add),
                             reads=["cU", "cCH", "consts"], writes=["cCH"])
                s.op("act", lambda e, j=j, cidx=cidx: e.activation(out=carry[:, cidx, :], in_=U[:, j, n:n + 2], func=AF.Copy),
                     reads=["cU"], writes=[ckey])
            self.proj("w_in", 0, KD, C_CB + cp * 256, 256, rhs[:1], ev_cb)
        self.off = grp0
        s.fence(["cCH", "cU"], ["mSGA", "mSGB", "mTA", "mTB"])
        SGA = r2(self.f32(2 * n)); SGB = r2(self.f32(2 * n)); TA = r2(self.f32(2 * n)); TB = r2(self.f32(2 * n))
        rhs_h = [(lambda k: hT[:, k, 0:n], n, ["hT"])]
        rhs_o = [(lambda k: oN[:, k, 0:n], n, ["oN"])]
        rhs_c = [(lambda k: cbz[:, k, 0:n], n, ["cbz"])]
        for jp in range(KD // 2):
            def ev_sig(dst, key):
                def f(g, oc, j, ps, pskey):
                    s.op("act", lambda e: e.activation(out=dst[:, j, :], in_=ps, func=AF.Sigmoid), reads=[pskey], writes=[key])
                return f

            def ev_a(g, oc, j, ps, pskey):
                s.op("dve", lambda e: e.tensor_tensor(out=TA[:, j, :], in0=ps, in1=SGA[:, j, :], op=ALU.mult),
                     reads=[pskey, "mSGA"], writes=["mTA"])

            def ev_b(g, oc, j, ps, pskey):
                s.op("dve", lambda e: e.tensor_tensor(out=TB[:, j, :], in0=ps, in1=SGB[:, j, :], op=ALU.mult),
                     reads=[pskey, "mSGB"], writes=["mTB"])
                mdst = merged[:, oc, 0:n]
                s.op("dve", lambda e: e.tensor_tensor(out=mdst, in0=TA[:, j, :], in1=TB[:, j, :], op=ALU.add),
                     reads=["mTA", "mTB"], writes=["merged"])

            self.proj("w_in", 0, KD, C_GA + jp * 256, 256, rhs_h, ev_sig(SGA, "mSGA"))
            self.proj("w_in", 0, KD, C_GB + jp * 256, 256, rhs_h, ev_sig(SGB, "mSGB"))
            self.proj("w_a", 0, NH, jp * 256, 256, rhs_o, ev_a)
            self.proj("w_b", 0, NH, jp * 256, 256, rhs_c, ev_b)
        self.proj("w_o", 0, KD, 0, D, [(lambda k: merged[:, k, 0:n], n, ["merged"])], self.ev_resid(n))

    def ev_resid(self, n):
        s = self.s

        def f(g, oc, j, ps, pskey):
            s.op("dve", lambda e: e.tensor_tensor(out=self.xT[:, oc, 0:n], in0=self.xT[:, oc, 0:n], in1=ps, op=ALU.add),
                 reads=[pskey, "xT"], writes=["xT"])
        return f

    def xattn(self, n):
        s = self.s
        xT, hT = self.xT, self.hT
        self.norm(xT, "xT", 1, n, hT, "hT")
        self.scratch()
        r3 = lambda ap, k: ap.rearrange("p (k t) -> p k t", k=k)
        qT = r3(self.bf(KD * n), KD)
        aT = r3(self.bf(KD * n), KD)
        Pexp = [self.f32(NMEM) for _ in range(2)]
        Pn = [self.bf(NMEM) for _ in range(2)]
        PTs = [r3(self.bf(2 * n), 2) for _ in range(2)]
        st = [self.f32(4) for _ in range(2)]

        def ev_q(g, oc, j, ps, pskey):
            s.op("act", lambda e: e.activation(out=qT[:, oc, 0:n], in_=ps, func=AF.Copy), reads=[pskey], writes=["qT"])
        self.proj("w_xq", 0, KD, 0, D, [(lambda k: hT[:, k, 0:n], n, ["hT"])], ev_q)
        scale = 1.0 / 32.0
        nblk = (n + P - 1) // P
        it = 0
        pbf = self.pbf.rearrange("p (i t) -> p i t", i=8)
        for hd in range(XH):
            pt = PTs[hd % 2]
            ptkey = ("PTs", hd % 2)
            for tb in range(nblk):
                rows = min(P, n - tb * P)
                bank = 5 + (it % 2)
                sc = self.psb[bank][0:rows, 0:NMEM]
                pe_, pn_, st_ = Pexp[it % 2], Pn[it % 2], st[it % 2]
                kx = it % 2
                for kc in range(8):
                    s.op("pe", lambda e, l=qT[:, 8 * hd + kc, tb * P:tb * P + rows], r=self.KT[:, 8 * hd + kc, :], a=(kc == 0), b=(kc == 7), sc=sc:
                         e.matmul(sc, lhsT=l, rhs=r, start=a, stop=b), reads=["qT", "KT"], writes=[("ps", bank)])
                s.op("dve", lambda e, sc=sc, st_=st_, rows=rows: e.tensor_reduce(out=st_[0:rows, 0:1], in_=sc, axis=AX.X, op=ALU.max),
                     reads=[("ps", bank)], writes=[("ast", kx)])
                s.op("dve", lambda e, st_=st_, rows=rows: e.tensor_scalar(out=st_[0:rows, 1:2], in0=st_[0:rows, 0:1], scalar1=-scale,
                                                                          scalar2=None, op0=ALU.mult),
                     reads=[("ast", kx)], writes=[("ast", kx)])
                s.op("act", lambda e, sc=sc, st_=st_, pe_=pe_, rows=rows: e.activation(
                    out=pe_[0:rows, :], in_=sc, func=AF.Exp, bias=st_[0:rows, 1:2], scale=scale, accum_out=st_[0:rows, 2:3]),
                     reads=[("ps", bank), ("ast", kx)], writes=[("ast", kx), ("Pexp", kx)])
                s.op("dve", lambda e, st_=st_, rows=rows: e.reciprocal(out=st_[0:rows, 3:4], in_=st_[0:rows, 2:3]),
                     reads=[("ast", kx)], writes=[("ast", kx)])
                s.op("dve", lambda e, st_=st_, pe_=pe_, pn_=pn_, rows=rows: e.tensor_scalar(
                    out=pn_[0:rows, :], in0=pe_[0:rows, :], scalar1=st_[0:rows, 3:4], scalar2=None, op0=ALU.mult),
                     reads=[("ast", kx), ("Pexp", kx)], writes=[("Pn", kx)])
                for mh in range(2):
                    s.op("pe", lambda e, o=pbf[:, 4 + (it % 2) * 2 + mh, 0:rows], a=pn_[0:rows, mh * P:(mh + 1) * P], rows=rows:
                         e.transpose(out=o, in_=a, identity=self.ident_b[0:rows, 0:rows]),
                         reads=[("Pn", kx), "consts"], writes=[("ps", 4)])
                s.op("act", lambda e, o=pt[:, :, tb * P:tb * P + rows], a=pbf[:, 4 + (it % 2) * 2:4 + (it % 2) * 2 + 2, 0:rows]:
                     e.activation(out=o, in_=a, func=AF.Copy), reads=[("ps", 4)], writes=[ptkey])
                it += 1
            for dc in range(8):
                half = dc % 2
                pv = self.psb[7][:, half * 256:half * 256 + n]
                for mh in range(2):
                    s.op("pe", lambda e, pv=pv, l=self.V[:, mh, hd * 1024 + dc * P:hd * 1024 + (dc + 1) * P], r=pt[:, mh, 0:n], a=(mh == 0), b=(mh == 1):
                         e.matmul(pv, lhsT=l, rhs=r, start=a, stop=b), reads=["V", ptkey], writes=[("ps", 7)])
                s.op("act", lambda e, pv=pv, o=aT[:, 8 * hd + dc, 0:n]: e.activation(out=o, in_=pv, func=AF.Copy),
                     reads=[("ps", 7)], writes=["aT"])
        self.proj("w_xo", 0, KD, 0, D, [(lambda k: aT[:, k, 0:n], n, ["aT"])], self.ev_resid(n))

    def ffn(self, n):
        s = self.s
        xT, hT = self.xT, self.hT
        self.norm(xT, "xT", 2, n, hT, "hT")
        self.scratch()
        hid = self.bf(KF * n).rearrange("p (k t) -> p k t", k=KF)
        GS = self.f32(2 * n).rearrange("p (j t) -> p j t", j=2)
        rhs = [(lambda k: hT[:, k, 0:n], n, ["hT"])]
        for hp in range(KF // 2):
            def ev_g(g, oc, j, ps, pskey):
                s.op("act", lambda e: e.activation(out=GS[:, j, :], in_=ps, func=AF.Silu), reads=[pskey], writes=["fGS"])

            def ev_u(g, oc, j, ps, pskey):
                s.op("dve", lambda e: e.tensor_tensor(out=hid[:, oc, 0:n], in0=ps, in1=GS[:, j, :], op=ALU.mult),
                     reads=[pskey, "fGS"], writes=["hid"])
            self.proj("w_gate", 0, KD, hp * 256, 256, rhs, ev_g)
            self.proj("w_up", 0, KD, hp * 256, 256, rhs, ev_u)
        self.proj("w_down", 0, KF, 0, D, [(lambda k: hid[:, k, 0:n], n, ["hid"])], self.ev_resid(n))

    def final_out(self, n, dst_rows):
        s = self.s
        xT = self.xT
        self.norm(xT, "xT", 4, n, xT, "xT")
        self.scratch()
        st = [self.f32(D) for _ in range(2)]
        nblk = (n + P - 1) // P
        for blk in range(nblk):
            rows = min(P, n - blk * P)
            stg = st[blk % 2]
            skey = ("xst", blk % 2)
            for cg in range(8):
                bank = 5 + (cg % 2)
                tp = self.psb[bank].rearrange("p (i t) -> p i t", i=4)
                for i in range(4):
                    c = 4 * cg + i
                    s.op("pe", lambda e, o=tp[0:rows, i, :], a=xT[:, c, blk * P:blk * P + rows]:
                         e.transpose(out=o, in_=a, identity=self.ident_f), reads=["xT", "consts"], writes=[("ps", bank)])
                self.copy(self.evac_engine(), stg[0:rows, cg * 512:(cg + 1) * 512].rearrange("p (i t) -> p i t", i=4),
                          tp[0:rows, :, :], reads=[("ps", bank)], writes=[skey])
            s.op("sp", lambda e, d=dst_rows[blk * P:blk * P + rows, :], a=stg[0:rows, :]: e.dma_start(out=d, in_=a),
                 reads=[skey], writes=[("out", self._nout())], dma_sem=self.sem_xst[blk % 2])

    def _nout(self):
        self._outc = getattr(self, "_outc", 0) + 1
        self.outkeys.append(("out", self._outc))
        return self._outc

    def setup(self):
        s = self.s
        nc = self.nc
        ld = lambda dst, src, key="consts": s.op("sp", lambda e: e.dma_start(out=dst, in_=src), writes=[key], dma_sem=self._csem())
        ld(self.ident_f, self.d_ident)
        ld(self.mask[0:64, :], self.d_mask)
        ld(self.gains.rearrange("p g k -> p (g k)"), self.d_gains)
        ld(self.lbl.rearrange("p r h -> p (r h)"), self.d_lbl)
        ld(self.hgn, self.d_hgn)
        ld(self.cw.rearrange("p t c -> p (t c)"), self.d_cw)
        s.op("act", lambda e: e.activation(out=self.ident_b, in_=self.ident_f, func=AF.Copy), reads=["consts"], writes=["consts"])
        s.op("dve", lambda e: e.memset(self.ones_b, 1.0), writes=["consts"])
        s.op("dve", lambda e: e.memset(self.rmask, 1.0), writes=["consts"])
        s.op("dve", lambda e: e.memset(self.rmask.rearrange("p (n c) -> p n c", c=64)[:, :, 0:1], 0.0), writes=["consts"])
        s.op("dve", lambda e: e.memset(self.rmask32, 1.0), writes=["consts"])
        s.op("dve", lambda e: e.memset(self.rmask32.rearrange("p (n c) -> p n c", c=32)[:, :, 0:1], 0.0), writes=["consts"])
        s.op("dve", lambda e: e.tensor_tensor(out=self.lb, in0=self.lbl[:, 0, :], in1=self.lbl[:, 1, :], op=ALU.subtract),
             reads=["consts"], writes=["consts"])
        s.op("act", lambda e: e.activation(out=self.lb, in_=self.lb, func=AF.Sigmoid), reads=["consts"], writes=["consts"])
        s.op("dve", lambda e: e.tensor_scalar(out=self.oml, in0=self.lb, scalar1=-1.0, scalar2=1.0, op0=ALU.mult, op1=ALU.add),
             reads=["consts"], writes=["consts"])
        self.mask_C = {64: self.mask[0:64, 0:64], 32: self.mask[0:32, 0:32]}
        self.rmask_C = {64: self.rmask, 32: self.rmask32}

    def _csem(self):
        self._ci = getattr(self, "_ci", -1) + 1
        return self.sem_c[self._ci % len(self.sem_c)]

    def zero_state(self):
        s = self.s
        s.op("dve", lambda e: e.memset(self.S.rearrange("p h v -> p (h v)"), 0.0), writes=["S"])
        s.op("dve", lambda e: e.memset(self.Sb.rearrange("p h v -> p (h v)"), 0.0), writes=["Sb"])

    def phase1_tile(self, src_rows):
        self.load_xT(src_rows, T, self.xT, "xT")
        self.norm(self.xT, "xT", 0, T, self.hT, "hT")
        self.scratch()
        h = self.hgrn_alloc(T, 64)
        for hg in range(NH // 2):
            self.hgrn_group(hg, T, 64, h, self.hT, True)

    def memkv(self):
        s = self.s
        self.load_xT(self.d_memp, NMEM, self.xT, "xT")
        self.norm(self.xT, "xT", 3, NMEM, self.hT, "hT")
        self.scratch()
        ktok = self.bf(2 * D).rearrange("p (b d) -> p b d", b=2)
        stg = [self.f32(SLABW) for _ in range(2)]
        it = 0
        for wname, dst_bf, okey, dout in (("w_xk", ktok, "ktok", self.o_mk), ("w_xv", self.V, "V", self.o_mv)):
            for cs in range(0, D, SLABW):
                slot = self.next_slab(wname, 0, KD, cs, SLABW)
                slab = self.ring[slot]
                self._acc_i = getattr(self, "_acc_i", 0) + 1
                acc, pskey = self.ps_acc(self._acc_i)
                for mb in range(2):
                    for k in range(KD):
                        s.op("pe", lambda e, o=acc[:, mb, :], l=self.hT[:, k, mb * P:(mb + 1) * P], r=slab[:, k, :], a=(k == 0), b=(k == KD - 1):
                             e.matmul(o, lhsT=l, rhs=r, start=a, stop=b), reads=[("ring", slot), "hT"], writes=[pskey])
                for mb in range(2):
                    sg = stg[it % 2]
                    skey = ("kst", it % 2)
                    s.op("act", lambda e, o=sg, a=acc[:, mb, :]: e.activation(out=o, in_=a, func=AF.Copy), reads=[pskey], writes=[skey])
                    s.op("dve", lambda e, o=dst_bf[:, mb, cs:cs + SLABW], a=acc[:, mb, :]: e.tensor_copy(out=o, in_=a),
                         reads=[pskey], writes=[okey])
                    s.op("sp", lambda e, d=dout[mb * P:(mb + 1) * P, cs:cs + SLABW], a=sg: e.dma_start(out=d, in_=a),
                         reads=[skey, ("xst", 1)], writes=[("out", self._nout())], dma_sem=self.sem_kst[it % 2])
                    it += 1
        self.k_transposes(ktok, "ktok", [("xst", 0)])

    def k_transposes(self, ktok, kkey, extra=()):
        s = self.s
        pbf = self.pbf.rearrange("p (i t) -> p i t", i=8)
        for c in range(KD):
            g = c % 2
            for mb in range(2):
                s.op("pe", lambda e, o=pbf[:, 4 * g + mb, :], a=ktok[:, mb, c * P:(c + 1) * P]:
                     e.transpose(out=o, in_=a, identity=self.ident_b), reads=[kkey, "consts"] + list(extra), writes=[("ps", 4)])
            self.copy(self.evac_engine(), self.KT[:, c, :].rearrange("p (b m) -> p b m", b=2), pbf[:, 4 * g:4 * g + 2, :],
                      reads=[("ps", 4)], writes=["KT"])

    def prompt_tile(self, t):
        n = T
        self.load_xT(self.d_xp[t * T:(t + 1) * T, :], n, self.xT, "xT")
        if t == 0:
            self.load_xT(self.d_xprev, 2, self.xprevT, "xprevT")
        if "mixer" in self.stages:
            self.mixer(n, 64, self.carry, "carry", with_prev=(t == 0))
        if self.debug and t == 0:
            self.dbg_dump(0)
        if "xattn" in self.stages:
            self.xattn(n)
        if self.debug and t == 0:
            self.dbg_dump(1)
        if "ffn" in self.stages:
            self.ffn(n)
        if self.debug and t == 0:
            self.dbg_dump(2)
        self.final_out(n, self.o_yp[t * T:(t + 1) * T, :])

    def dbg_dump(self, i):
        s = self.s
        s.op("sp", lambda e: e.dma_start(out=self.o_dbg[:, i * KD * T:(i + 1) * KD * T], in_=self.xT.rearrange("p k t -> p (k t)")),
             reads=["xT"], writes=[("out", self._nout())], dma_sem=self.sem_misc[i])

    def store_small(self, dst, src, key, semi):
        self.s.op("sp", lambda e: e.dma_start(out=dst, in_=src), reads=[key], writes=[("out", self._nout())], dma_sem=self.sem_misc[semi])

    def sample_pass(self):
        s = self.s
        n = TS
        s.op("sp", lambda e: e.dma_start(out=self.S.rearrange("p h v -> p (h v)"), in_=self.d_sh), writes=["S"], dma_sem=self.sem_misc[4])
        s.op("act", lambda e: e.activation(out=self.Sb.rearrange("p h v -> p (h v)"), in_=self.S.rearrange("p h v -> p (h v)"), func=AF.Copy),
             reads=["S"], writes=["Sb"])
        s.op("sp", lambda e: e.dma_start(out=self.carry_s.rearrange("p c r -> p (c r)"), in_=self.d_sc), writes=["carry_s"], dma_sem=self.sem_misc[5])
        self.load_xT(self.d_xs, n, self.xT, "xT")
        if "mixer" in self.stages:
            self.mixer(n, 32, self.carry_s, "carry_s", with_prev=False)
        self.scratch()
        self.off = self.scratch0 + 20 * 1024
        ktok = self.bf(2 * D).rearrange("p (b d) -> p b d", b=2)
        s.op("pool", lambda e: e.dma_start(out=ktok, in_=self.d_cmk.rearrange("(b p) d -> p b d", p=P)),
             writes=["ktok_s", ("xst", 0), ("xst", 1)], dma_sem=self.sem_misc[6], after_all=True)
        s.op("pool", lambda e: e.dma_start(out=self.V, in_=self.d_cmv.rearrange("(b p) d -> p b d", p=P)),
             writes=["V"], dma_sem=self.sem_misc[7])
        self.k_transposes(ktok, "ktok_s")
        if "xattn" in self.stages:
            self.xattn(n)
        if "ffn" in self.stages:
            self.ffn(n)
        self.final_out(n, self.o_ys)

    def program(self):
        self.outkeys = []
        self._outc = 0
        self._acc_i = 0
        self._alt = 0
        self._ci = -1
        self.setup()
        self.zero_state()
        np1 = self.NP1 if "p1" in self.stages else 0
        self.in_p1 = np1 > 0
        self.p1_loads_left = np1 * 16
        for t in range(np1):
            self.phase1_tile(self.d_xpred[t * T:(t + 1) * T, :])
        self.in_p1 = False
        if not self.s.dry:
            self.conv_fence()
        s = self.s
        if self.NP1 > 0:
            s.op("act", lambda e: e.activation(out=self.Sb.rearrange("p h v -> p (h v)"), in_=self.S.rearrange("p h v -> p (h v)"), func=AF.Copy),
                 reads=["S"], writes=["Sb"])
        if "memkv" in self.stages:
            self.memkv()
        for t in range(self.NT if "tile" in self.stages else 0):
            self.prompt_tile(t)
        self.store_small(self.o_hp, self.S.rearrange("p h v -> p (h v)"), "S", 0)
        self.store_small(self.o_cp, self.carry.rearrange("p c r -> p (c r)"), "carry", 1)
        if "sample" in self.stages:
            self.sample_pass()
        self.store_small(self.o_hs, self.S.rearrange("p h v -> p (h v)"), "S", 2)
        self.store_small(self.o_cs, self.carry_s.rearrange("p c r -> p (c r)"), "carry_s", 3)
        s.op("sp", lambda e: e.wait_ge(self.sem_misc[0], 0), reads=list(self.outkeys))

    def build(self):
        nc = self.nc
        s = self.s
        s.dry = True
        self.wc_count = {}
        self.in_p1 = False
        self.conv_todo = []
        self.wc_pre = set()
        self.wc_off = {}
        self.wc_written = set()
        self.conv_n = 0
        self.program()
        plan = s.plan
        self.wc_off = {}
        CAP = 720 * 1024
        sizes = [0]
        for spec in plan:
            if self.wc_count[spec] > 1 and spec not in self.wc_off:
                n_ = spec[2] * spec[4]
                if sizes[-1] + n_ > CAP:
                    sizes.append(0)
                self.wc_off[spec] = (len(sizes) - 1, sizes[-1])
                sizes[-1] += n_
        self.wc_written = set()
        self.wc_pre = set()
        self.conv_n = 0
        p1_specs = set(plan[:(self.NP1 if "p1" in self.stages else 0) * 16])
        self.conv_todo = [sp_ for sp_ in self.wc_off if sp_ not in p1_specs] if p1_specs else []
        self.wcaches = [nc.dram_tensor(f"wcache{i}", [P, max(sz, 1)], BF16).ap() for i, sz in enumerate(sizes) if sz]
        s.dry = False
        s.reset()
        s.plan = plan
        self.program()
        assert s.plan_pos == len(plan), (s.plan_pos, len(plan))
        run = s.emit(nc, None, self.sems)
        with nc.Block() as block:
            @block.tensor
            def _(e):
                run("pe", e)

            @block.scalar
            def _(e):
                run("act", e)

            @block.vector
            def _(e):
                run("dve", e)

            @block.gpsimd
            def _(e):
                run("pool", e)

            @block.sync
            def _(e):
                run("sp", e)
        return nc


def _pc(v):
    v = np.asarray(v, dtype=np.float32)
    return np.ascontiguousarray(v.reshape(-1, P).T)


def make_in_maps(inp, NT, wnames):
    SEG = NT * T
    xpr = np.asarray(inp["x_prompt"], dtype=np.float32)
    ident = np.eye(P, dtype=np.float32)
    maskc = np.triu(np.ones((64, 64), dtype=np.float32))
    gains = np.stack([_pc(inp["norm_mix"][0]), _pc(inp["norm_xattn"][0]), _pc(inp["norm_ffn"][0]),
                      _pc(inp["norm_mem"][0]), _pc(inp["norm_final"])], axis=1).reshape(P, 5 * KD)
    lbl = np.ascontiguousarray(np.asarray(inp["lb_logits"], np.float32).reshape(2, NH, P).transpose(2, 0, 1)).reshape(P, 2 * NH)
    hgn = _pc(inp["hg_norm"][0])
    cw = np.ascontiguousarray(np.asarray(inp["conv_w"][0], np.float32).reshape(3, NH, P).transpose(2, 0, 1)).reshape(P, 3 * NH)
    shared = {"ident": ident, "maskc": maskc, "gains": np.ascontiguousarray(gains), "lbl": lbl, "hgn": hgn, "cw": cw}
    for n in wnames:
        shared[n] = np.ascontiguousarray(np.asarray(inp[n][0], dtype=np.float32))
    maps = []
    for c in range(8):
        b, j = c // 4, c % 4
        m = dict(shared)
        m["xp"] = np.ascontiguousarray(xpr[b, j * SEG:(j + 1) * SEG])
        xpred = np.zeros((3 * SEG, D), np.float32)
        if j > 0:
            xpred[(3 - j) * SEG:] = xpr[b, 0:j * SEG]
        m["xpred"] = xpred
        xprev = np.zeros((2, D), np.float32)
        if j > 0:
            xprev[:] = xpr[b, j * SEG - 2:j * SEG]
        m["xprev"] = xprev
        m["xs"] = np.ascontiguousarray(np.asarray(inp["x_sample"][c], np.float32))
        m["cmk"] = np.ascontiguousarray(np.asarray(inp["cache_mem_k"][0, c], np.float32).reshape(NMEM, D))
        m["cmv"] = np.ascontiguousarray(np.asarray(inp["cache_mem_v"][0, c], np.float32).reshape(NMEM, D))
        m["sh"] = np.ascontiguousarray(np.asarray(inp["state_hgrn"][0, c], np.float32).transpose(1, 0, 2)).reshape(P, NH * 128)
        m["sc"] = np.ascontiguousarray(np.asarray(inp["state_conv"][0, c], np.float32).reshape(2, NH, P).transpose(2, 1, 0)).reshape(P, 32)
        m["memp"] = np.ascontiguousarray(np.asarray(inp["mem_prompt"][b], np.float32))
        maps.append(m)
    return maps


def assemble(results, NT):
    SEG = NT * T
    L = 4 * SEG
    y_prompt = np.zeros((2, L, D), np.float32)
    y_sample = np.zeros((8, TS, D), np.float32)
    mk = np.zeros((1, 2, NMEM, XH, D // XH), np.float32)
    mv = np.zeros((1, 2, NMEM, XH, D // XH), np.float32)
    hp = np.zeros((1, 2, NH, 128, 128), np.float32)
    cp = np.zeros((1, 2, 2, HW), np.float32)
    hs = np.zeros((1, 8, NH, 128, 128), np.float32)
    cs = np.zeros((1, 8, 2, HW), np.float32)
    st = lambda a: np.asarray(a).reshape(P, NH, 128).transpose(1, 0, 2)
    cv = lambda a: np.asarray(a).reshape(P, NH, 2).transpose(2, 1, 0).reshape(2, HW)
    for c in range(8):
        b, j = c // 4, c % 4
        r = results[c]
        y_prompt[b, j * SEG:(j + 1) * SEG] = r["yp"]
        y_sample[c] = r["ys"]
        hs[0, c] = st(r["hs"])
        cs[0, c] = cv(r["cs"])
        if j == 0:
            mk[0, b] = np.asarray(r["mk"]).reshape(NMEM, XH, D // XH)
            mv[0, b] = np.asarray(r["mv"]).reshape(NMEM, XH, D // XH)
        if j == 3:
            hp[0, b] = st(r["hp"])
            cp[0, b] = cv(r["cp"])
    return (y_prompt, y_sample, mk, mv, hp, cp, hs, cs)


def run(inp, NT, debug=False, stages=None):
    import time, sys
    t0 = time.time()
    bld = Builder(NT, 3 * NT, debug=debug, stages=stages)
    nc = bld.build()
    t1 = time.time()
    maps = make_in_maps(inp, NT, list(bld.w.keys()))
    t2 = time.time()
    res = run_bass_kernel_spmd(nc, maps, core_ids=list(range(8)))
    print(f"[kernel] build {t1 - t0:.1f}s maps {t2 - t1:.1f}s run {time.time() - t2:.1f}s nops={ {e: len(v) for e, v in bld.s.ops.items()} }", file=sys.stderr)
    outs = assemble(res.results, NT)
    if debug:
        return outs, [r["dbg"] for r in res.results]
    return outs


def kernel(**inputs):
    return run(inputs, 8)
```
